# Optimizing a Trainium2 kernel written in Bass

```python
import jax, jax.numpy as jnp
from jax import lax
import numpy as np

D_MODEL = 1024
BATCH = 4
SEQ = 4096
DEPTH = 2
DEC_BATCH = 32
DEC_SEQ = 8
PAST_LEN = 8192
PAGE_SIZE = 128

N_MIXERS = 2
N_CONV_LAYERS = (DEPTH + N_MIXERS - 1) // N_MIXERS
N_SB_LAYERS = DEPTH // N_MIXERS
CONV_WIDTH = 31
CONV_STATE = CONV_WIDTH - 1
N_HEADS = 16
HEAD_DIM = D_MODEL // N_HEADS
Q_BLOCK = 128
D_FF = -(-8 * D_MODEL // (3 * 256)) * 256
RMS_EPS = 1e-6
LN_EPS = 1e-5
SB_BIAS_INIT = -7.0

kernel_name = "conformer_conv_stickbreaking_hybrid_step"


def rmsnorm(x, g):
    x32 = x.astype(jnp.float32)
    y = x32 * lax.rsqrt(jnp.mean(x32 * x32, axis=-1, keepdims=True) + RMS_EPS)
    return (y * g.astype(jnp.float32)).astype(x.dtype)


def layernorm(x, g, b):
    x32 = x.astype(jnp.float32)
    mu = jnp.mean(x32, axis=-1, keepdims=True)
    var = jnp.mean(jnp.square(x32 - mu), axis=-1, keepdims=True)
    y = (x32 - mu) * lax.rsqrt(var + LN_EPS) * g.astype(jnp.float32) + b.astype(jnp.float32)
    return y.astype(x.dtype)


def swiglu_ffn(h, w_gate, w_up, w_down):
    return (jax.nn.silu(h @ w_gate) * (h @ w_up)) @ w_down


def conformer_conv(h, prefix, w_pw1, b_pw1, w_dw, b_dw, ln_g, ln_b, w_pw2, b_pw2):
    u = h @ w_pw1 + b_pw1
    a, gate = jnp.split(u, 2, axis=-1)
    u = a * jax.nn.sigmoid(gate)
    u_ext = jnp.concatenate([prefix.astype(u.dtype), u], axis=1)
    c = lax.conv_general_dilated(
        u_ext, w_dw[:, None, :].astype(u.dtype), window_strides=(1,), padding='VALID',
        dimension_numbers=('NWC', 'WIO', 'NWC'), feature_group_count=D_MODEL) + b_dw
    c = jax.nn.silu(layernorm(c, ln_g, ln_b))
    return c @ w_pw2 + b_pw2, u_ext[:, -CONV_STATE:]


def sb_attend(q, k, v, q_pos, bias):
    tk = k.shape[1]
    z = jnp.einsum('bthd,bshd->bhts', q.astype(jnp.float32), k.astype(jnp.float32)) * (HEAD_DIM ** -0.5)
    z = z + bias.astype(jnp.float32)[None, :, None, None]
    causal = jnp.arange(tk, dtype=jnp.int32)[None, :] < q_pos[:, None]
    log_1m_beta = jnp.where(causal, -jax.nn.softplus(z), 0.0)
    after = lax.cumsum(log_1m_beta, axis=3, reverse=True) - log_1m_beta
    a = jnp.where(causal, jnp.exp(jax.nn.log_sigmoid(z) + after), 0.0)
    o = jnp.einsum('bhts,bshd->bthd', a, v.astype(jnp.float32))
    return o.astype(q.dtype)


def sb_prompt_attention(q, k, v, bias):
    b, t = q.shape[0], q.shape[1]
    nb = t // Q_BLOCK
    q_blocks = q.reshape(b, nb, Q_BLOCK, N_HEADS, HEAD_DIM).swapaxes(0, 1)
    pos = jnp.arange(t, dtype=jnp.int32).reshape(nb, Q_BLOCK)
    o = lax.map(lambda qp: sb_attend(qp[0], k, v, qp[1], bias), (q_blocks, pos))
    return o.swapaxes(0, 1).reshape(b, t, N_HEADS * HEAD_DIM)


def sb_sample_attention(q, k_new, v_new, k_pool, v_pool, page_table, bias):
    db, n_pages = page_table.shape
    past = n_pages * k_pool.shape[1]
    k_past = k_pool[page_table].reshape(db, past, N_HEADS, HEAD_DIM)
    v_past = v_pool[page_table].reshape(db, past, N_HEADS, HEAD_DIM)
    k_all = jnp.concatenate([k_past.astype(k_new.dtype), k_new], axis=1)
    v_all = jnp.concatenate([v_past.astype(v_new.dtype), v_new], axis=1)
    q_pos = past + jnp.arange(q.shape[1], dtype=jnp.int32)
    return sb_attend(q, k_all, v_all, q_pos, bias).reshape(db, q.shape[1], N_HEADS * HEAD_DIM)


def split_qkv(h, w_qkv):
    b, t = h.shape[0], h.shape[1]
    qkv = (h @ w_qkv).reshape(b, t, 3, N_HEADS, HEAD_DIM)
    return qkv[:, :, 0], qkv[:, :, 1], qkv[:, :, 2]


def setup_inputs(seed: int = 0) -> dict:
    key = jax.random.key(seed)
    ks = jax.random.split(key, 24)
    f32 = jnp.float32
    n_pages = PAST_LEN // PAGE_SIZE
    n_pool = (DEC_BATCH * n_pages * 5) // 4
    nrm = lambda k, shape, s: jax.random.normal(k, shape, f32) * s
    page_table = jax.random.permutation(ks[5], n_pool)[:DEC_BATCH * n_pages].reshape(DEC_BATCH, n_pages).astype(jnp.int32)
    return {
        "x_prompt": nrm(ks[0], (BATCH, SEQ, D_MODEL), 1.0),
        "x_sample": nrm(ks[1], (DEC_BATCH, DEC_SEQ, D_MODEL), 1.0),
        "cache_conv": nrm(ks[2], (N_CONV_LAYERS, DEC_BATCH, CONV_STATE, D_MODEL), 0.5),
        "cache_k": nrm(ks[3], (N_SB_LAYERS, n_pool, PAGE_SIZE, N_HEADS, HEAD_DIM), 1.0),
        "cache_v": nrm(ks[4], (N_SB_LAYERS, n_pool, PAGE_SIZE, N_HEADS, HEAD_DIM), 1.0),
        "page_table": page_table,
        "mix_norm_g": 1.0 + nrm(ks[6], (DEPTH, D_MODEL), 0.05),
        "ffn_norm_g": 1.0 + nrm(ks[7], (DEPTH, D_MODEL), 0.05),
        "final_norm_g": 1.0 + nrm(ks[8], (D_MODEL,), 0.05),
        "cv_w_pw1": nrm(ks[9], (N_CONV_LAYERS, D_MODEL, 2 * D_MODEL), D_MODEL ** -0.5),
        "cv_b_pw1": nrm(ks[10], (N_CONV_LAYERS, 2 * D_MODEL), 0.02),
        "cv_w_dw": nrm(ks[11], (N_CONV_LAYERS, CONV_WIDTH, D_MODEL), CONV_WIDTH ** -0.5),
        "cv_b_dw": nrm(ks[12], (N_CONV_LAYERS, D_MODEL), 0.02),
        "cv_ln_g": 1.0 + nrm(ks[13], (N_CONV_LAYERS, D_MODEL), 0.05),
        "cv_ln_b": nrm(ks[14], (N_CONV_LAYERS, D_MODEL), 0.02),
        "cv_w_pw2": nrm(ks[15], (N_CONV_LAYERS, D_MODEL, D_MODEL), D_MODEL ** -0.5),
        "cv_b_pw2": nrm(ks[16], (N_CONV_LAYERS, D_MODEL), 0.02),
        "sb_w_qkv": nrm(ks[17], (N_SB_LAYERS, D_MODEL, 3 * N_HEADS * HEAD_DIM), D_MODEL ** -0.5),
        "sb_w_o": nrm(ks[18], (N_SB_LAYERS, N_HEADS * HEAD_DIM, D_MODEL), (N_HEADS * HEAD_DIM) ** -0.5),
        "sb_logit_bias": SB_BIAS_INIT + nrm(ks[22], (N_SB_LAYERS, N_HEADS), 0.1),
        "ffn_w_gate": nrm(ks[19], (DEPTH, D_MODEL, D_FF), D_MODEL ** -0.5),
        "ffn_w_up": nrm(ks[20], (DEPTH, D_MODEL, D_FF), D_MODEL ** -0.5),
        "ffn_w_down": nrm(ks[21], (DEPTH, D_FF, D_MODEL), D_FF ** -0.5),
    }


def reference(x_prompt, x_sample, cache_conv, cache_k, cache_v, page_table,
              mix_norm_g, ffn_norm_g, final_norm_g,
              cv_w_pw1, cv_b_pw1, cv_w_dw, cv_b_dw, cv_ln_g, cv_ln_b, cv_w_pw2, cv_b_pw2,
              sb_w_qkv, sb_w_o, sb_logit_bias, ffn_w_gate, ffn_w_up, ffn_w_down):
    xp, xs = x_prompt, x_sample
    bp, tp = xp.shape[0], xp.shape[1]
    conv_p, conv_s, kp_list, vp_list, ks_list, vs_list = [], [], [], [], [], []
    for i in range(DEPTH):
        j = i // N_MIXERS
        hp = rmsnorm(xp, mix_norm_g[i])
        hs = rmsnorm(xs, mix_norm_g[i])
        if i % N_MIXERS == 0:
            cv = (cv_w_pw1[j], cv_b_pw1[j], cv_w_dw[j], cv_b_dw[j], cv_ln_g[j], cv_ln_b[j], cv_w_pw2[j], cv_b_pw2[j])
            fresh = jnp.zeros((bp, CONV_STATE, D_MODEL), xp.dtype)
            op, st_p = conformer_conv(hp, fresh, *cv)
            os_, st_s = conformer_conv(hs, cache_conv[j], *cv)
            conv_p.append(st_p)
            conv_s.append(st_s)
        else:
            q_p, k_p, v_p = split_qkv(hp, sb_w_qkv[j])
            op = sb_prompt_attention(q_p, k_p, v_p, sb_logit_bias[j]) @ sb_w_o[j]
            kp_list.append(k_p.reshape(bp, tp // PAGE_SIZE, PAGE_SIZE, N_HEADS, HEAD_DIM))
            vp_list.append(v_p.reshape(bp, tp // PAGE_SIZE, PAGE_SIZE, N_HEADS, HEAD_DIM))
            q_s, k_s, v_s = split_qkv(hs, sb_w_qkv[j])
            os_ = sb_sample_attention(q_s, k_s, v_s, cache_k[j], cache_v[j], page_table, sb_logit_bias[j]) @ sb_w_o[j]
            ks_list.append(k_s)
            vs_list.append(v_s)
        xp = xp + op
        xs = xs + os_
        xp = xp + swiglu_ffn(rmsnorm(xp, ffn_norm_g[i]), ffn_w_gate[i], ffn_w_up[i], ffn_w_down[i])
        xs = xs + swiglu_ffn(rmsnorm(xs, ffn_norm_g[i]), ffn_w_gate[i], ffn_w_up[i], ffn_w_down[i])
    y_prompt = rmsnorm(xp, final_norm_g)
    y_sample = rmsnorm(xs, final_norm_g)
    return (y_prompt, y_sample, jnp.stack(conv_p), jnp.stack(conv_s),
            jnp.stack(kp_list), jnp.stack(vp_list), jnp.stack(ks_list), jnp.stack(vs_list))
```

```python
import numpy as np
import concourse.bass as bass
import concourse.mybir as mybir
from concourse.bass_utils import run_bass_kernel_spmd

F32 = mybir.dt.float32
BF16 = mybir.dt.bfloat16
I32 = mybir.dt.int32
AF = mybir.ActivationFunctionType
ALU = mybir.AluOpType

D = 1024
DFF = 2816
NH = 16
KC = 8
FC = 22
CW = 31
CS = 30
TW = 512
RMS_EPS = 1e-6
LN_EPS = 1e-5
NSS = 4
DSQ = 8
NS = NSS * DSQ


class Buf:
    __slots__ = ("name", "w", "rs", "dsem", "dcnt", "ssem", "scnt")

    def __init__(self, name):
        self.name = name
        self.w = None
        self.rs = []
        self.dsem = None
        self.dcnt = 0
        self.ssem = None
        self.scnt = 0


class Sched:
    def __init__(self, nc):
        self.nc = nc
        self.eng = {"pe": nc.tensor, "act": nc.scalar, "dve": nc.vector, "pool": nc.gpsimd, "sp": nc.sync}
        self.sem = {}
        self.cnt = {}
        self._ctx = []
        for k in self.eng:
            cm = nc.semaphore("prog_" + k)
            self.sem[k] = cm.__enter__()
            self._ctx.append(cm)
            self.cnt[k] = 0
        self.waited = {k: {} for k in self.eng}
        self.pending = {k: [] for k in self.eng}
        self.out_waits = []
        self.nsem = 0
        self.dma_latest = {}

    def new_sem(self, name):
        cm = self.nc.semaphore(name)
        s = cm.__enter__()
        self._ctx.append(cm)
        self.nsem += 1
        return s

    def close(self):
        for cm in reversed(self._ctx):
            cm.__exit__(None, None, None)

    def _wait(self, ek, sem, val):
        d = self.waited[ek]
        key = id(sem)
        if d.get(key, 0) >= val:
            return
        d[key] = val
        self.eng[ek].wait_ge(sem, val)

    def _deps(self, ek, reads, writes):
        for b in reads:
            if b.w is not None:
                self._wait(ek, b.w[0], b.w[1])
        for b in writes:
            if b.w is not None and b.w[2] != ek:
                self._wait(ek, b.w[0], b.w[1])
            for r in b.rs:
                if r[2] != ek:
                    self._wait(ek, r[0], r[1])

    def op(self, ek, fn, reads=(), writes=(), inc=True):
        self._deps(ek, reads, writes)
        ins = fn()
        if inc:
            self.cnt[ek] += 1
            ins.then_inc(self.sem[ek], 1)
            tag = (self.sem[ek], self.cnt[ek], ek)
            for b in self.pending[ek]:
                b.rs.append(tag)
            self.pending[ek] = []
            for b in reads:
                b.rs.append(tag)
            for b in writes:
                b.w = tag
                b.rs = []
        else:
            self.pending[ek].extend(reads)
            self.pending[ek].extend(writes)
        return ins

    def dma(self, qk, out_ap, in_ap, reads=(), writes=(), indirect=None):
        self._deps(qk, reads, writes)
        e = self.eng[qk]
        if indirect is not None:
            ins = e.indirect_dma_start(out=out_ap, out_offset=None, in_=in_ap, in_offset=indirect)
        else:
            ins = e.dma_start(out=out_ap, in_=in_ap)
        if writes:
            b = writes[0]
            if b.dsem is None:
                b.dsem = self.new_sem("d_" + b.name)
            b.dcnt += 1
            ins.then_inc(b.dsem, 16)
            tag = (b.dsem, 16 * b.dcnt, "dma")
            self.dma_latest[id(b.dsem)] = (b.dsem, 16 * b.dcnt)
            for wb in writes:
                wb.w = tag
                wb.rs = []
            for rb in reads:
                rb.rs.append(tag)
        else:
            b = reads[0]
            if b.ssem is None:
                b.ssem = self.new_sem("s_" + b.name)
            b.scnt += 1
            ins.then_inc(b.ssem, 16)
            tag = (b.ssem, 16 * b.scnt, "dma")
            self.dma_latest[id(b.ssem)] = (b.ssem, 16 * b.scnt)
            for rb in reads:
                rb.rs.append(tag)
            self.out_waits.append(tag)
        return ins

    def barrier(self):
        for ek in self.eng:
            for o in self.eng:
                if o != ek and self.cnt[o] > 0:
                    self._wait(ek, self.sem[o], self.cnt[o])
            for sem, val in self.dma_latest.values():
                self._wait(ek, sem, val)

    def finish(self, ek="sp"):
        last = {}
        for sem, val, _ in self.out_waits:
            k = id(sem)
            if k not in last or last[k][1] < val:
                last[k] = (sem, val)
        for sem, val in last.values():
            self._wait(ek, sem, val)


def build(SEQ, NPG, NPOOL, stage=99, sample=True):
    nc = bass.Bass("TRN2", target_bir_lowering=False)
    NT = SEQ // TW
    NBLK = SEQ // 128

    def din(name, shape, dt=F32):
        return nc.dram_tensor(name, list(shape), dt, kind="ExternalInput").ap()

    def dout(name, shape, dt=F32):
        return nc.dram_tensor(name, list(shape), dt, kind="ExternalOutput").ap()

    xp = din("xp", [SEQ, D])
    xs = din("xs", [NS, D])
    cconv = din("cconv", [NSS * CS, D])
    ck = din("ck", [NPOOL * 128, D])
    cv = din("cv", [NPOOL * 128, D])
    pt = din("pt", [NSS, NPG], I32)
    vec_in = {
        "mix_g": din("mix_g", [2, D]), "ffn_g": din("ffn_g", [2, D]), "fin_g": din("fin_g", [1, D]),
        "b_pw1": din("b_pw1", [2, D]), "b_dw": din("b_dw", [1, D]), "ln_g": din("ln_g", [1, D]),
        "ln_b": din("ln_b", [1, D]), "b_pw2": din("b_pw2", [1, D]), "w_dw": din("w_dw", [CW, D]),
    }
    sbb = din("sbb", [1, NH])
    w_pw1 = din("w_pw1", [D, 2 * D])
    w_pw2 = din("w_pw2", [D, D])
    w_qkv = din("w_qkv", [D, 3 * D])
    w_o = din("w_o", [D, D])
    wg = din("wg", [2, D, DFF])
    wu = din("wu", [2, D, DFF])
    wd = din("wd", [2, DFF, D])

    yp = dout("yp", [SEQ, D])
    ys = dout("ys", [NS, D])
    csp = dout("csp", [CS, D])
    css = dout("css", [NSS * CS, D])
    kp = dout("kp", [SEQ, D])
    vp = dout("vp", [SEQ, D])
    ks = dout("ks", [NS, D])
    vs = dout("vs", [NS, D])

    def scr(name, shape, dt):
        return nc.dram_tensor(name, list(shape), dt).ap()

    wsc = {}
    wsc["pw1"] = scr("wsc_pw1", [4, 128, 8, 512], BF16)
    wsc["pw2"] = scr("wsc_pw2", [2, 128, 8, 512], BF16)
    wsc["qkv"] = scr("wsc_qkv", [6, 128, 8, 512], BF16)
    wsc["wo"] = scr("wsc_wo", [2, 128, 8, 512], BF16)
    for l in range(2):
        wsc["wg%d" % l] = scr("wsc_wg%d" % l, [11, 128, 8, 256], BF16)
        wsc["wu%d" % l] = scr("wsc_wu%d" % l, [11, 128, 8, 256], BF16)
        wsc["wd%d" % l] = scr("wsc_wd%d" % l, [2, 3, 128, 8, 512], BF16)
    x1T = scr("x1T", [KC, 128, SEQ], F32)
    qTs = scr("qTs", [KC, 128, SEQ], BF16)
    kTs = scr("kTs", [KC, 128, SEQ], BF16)
    vtok = scr("vtok", [SEQ, D], BF16)
    oTs = scr("oTs", [KC, 128, SEQ], BF16)

    S = Sched(nc)
    B_wsc = {k: Buf("wsc_" + k) for k in wsc}
    B_x1T = [Buf("x1T%d" % c) for c in range(KC)]
    B_qTs = [Buf("qTs%d" % c) for c in range(KC)]
    B_kTs = [Buf("kTs%d" % c) for c in range(KC)]
    B_vtok = Buf("vtok")
    B_oTs = [Buf("oTs%d" % c) for c in range(KC)]

    from contextlib import ExitStack
    es = ExitStack()

    def sb(name, shape, dt, stack=None):
        return (stack or es).enter_context(nc.sbuf_tensor(name, list(shape), dt))

    def psum(name):
        return es.enter_context(nc.psum_tensor(name, [128, 512], F32))

    ident = sb("ident", [128, 128], F32)
    identb = sb("identb", [128, 128], BF16)
    trim = sb("trim", [128, 128], BF16)
    tri8 = sb("tri8", [128, 128], BF16)
    one8 = sb("one8", [128, 128], BF16)
    onesD = sb("onesD", [128, 128], BF16)
    scratchf = sb("scratchf", [128, 128], F32)
    vrows = sb("vrows", [128, 3, 128], F32)
    vT = sb("vT", [128, 3 * 128], F32)
    bias_h = sb("bias_h", [128, NH], F32)
    B_const = Buf("const")
    B_scr = Buf("scratchf")
    B_vrows = Buf("vrows")
    B_vT = Buf("vT")
    B_biash = Buf("bias_h")
    PS = [psum("ps%d" % i) for i in range(8)]
    B_PS = [Buf("ps%d" % i) for i in range(8)]

    g = nc.gpsimd
    S.op("pool", lambda: g.memset(ident[:], 1.0), writes=[B_const])
    S.op("pool", lambda: g.affine_select(out=ident[:], in_=ident[:], pattern=[[-1, 128]], compare_op=ALU.is_equal,
                                         fill=0.0, base=0, channel_multiplier=1), reads=[B_const], writes=[B_const])
    S.op("pool", lambda: g.tensor_copy(out=identb[:], in_=ident[:]), reads=[B_const], writes=[B_const])
    S.op("pool", lambda: g.memset(scratchf[:], 1.0), writes=[B_scr])
    S.op("pool", lambda: g.affine_select(out=scratchf[:], in_=scratchf[:], pattern=[[1, 128]], compare_op=ALU.is_gt,
                                         fill=0.0, base=0, channel_multiplier=-1), reads=[B_scr], writes=[B_scr])
    S.op("pool", lambda: g.tensor_copy(out=trim[:], in_=scratchf[:]), reads=[B_scr], writes=[B_const])
    S.op("pool", lambda: g.memset(scratchf[:], -8.0), reads=[B_scr], writes=[B_scr])
    S.op("pool", lambda: g.affine_select(out=scratchf[:], in_=scratchf[:], pattern=[[-1, 128]], compare_op=ALU.is_ge,
                                         fill=0.0, base=0, channel_multiplier=1), reads=[B_scr], writes=[B_scr])
    S.op("pool", lambda: g.tensor_copy(out=tri8[:], in_=scratchf[:]), reads=[B_scr], writes=[B_const])
    S.op("pool", lambda: g.memset(one8[:], -8.0), writes=[B_const])
    S.op("pool", lambda: g.memset(onesD[:], 1.0 / D), writes=[B_const])

    VBASE = {}
    r = 0
    for name, nrows in [("mix_g", 2), ("ffn_g", 2), ("fin_g", 1), ("b_pw1", 2), ("b_dw", 1), ("ln_g", 1),
                        ("ln_b", 1), ("b_pw2", 1), ("w_dw", CW)]:
        VBASE[name] = r
        src = vec_in[name].rearrange("r (c p) -> (r c) p", p=128)
        n = nrows * 8
        o = 0
        while o < n:
            grp, pp = divmod(r + o, 128)
            take = min(n - o, 128 - pp)
            S.dma("sp", vrows[pp:pp + take, grp, :], src[o:o + take, :], writes=[B_vrows])
            o += take
        r += n
    NVR = r
    S.op("pool", lambda: g.memset(vT[:], 0.0), writes=[B_vT])
    for grp in range(3):
        rows = min(128, NVR - grp * 128)
        S.op("pe", lambda grp=grp, rows=rows: nc.tensor.transpose(PS[7][:, 0:rows], vrows[0:rows, grp, :], ident[0:rows, 0:rows]),
             reads=[B_vrows, B_const], writes=[B_PS[7]])
        S.op("dve", lambda grp=grp, rows=rows: nc.vector.tensor_copy(out=vT[:, grp * 128:grp * 128 + rows], in_=PS[7][:, 0:rows]),
             reads=[B_PS[7]], writes=[B_vT])
    S.dma("sp", bias_h[:], sbb[0:1, :].partition_broadcast(128), writes=[B_biash])

    def vcol(name, row, c):
        k = VBASE[name] + row * 8 + c
        return vT[:, k:k + 1]

    with ExitStack() as st0:
        stg = [sb("stg%d" % i, [128, 3072], BF16, st0) for i in range(3)]
        B_stg = [Buf("stg%d" % i) for i in range(3)]
        si = [0]

        def precast(key, wap, din_rows, dcols, cb, down=False):
            for kc in range(din_rows // 128):
                i = si[0] % 3
                si[0] += 1
                S.dma("pool", stg[i][:, 0:dcols], wap[kc * 128:(kc + 1) * 128, :], writes=[B_stg[i]])
                if down:
                    kg, kcl = divmod(kc, 8)
                    dst = wsc[key][:, kg, :, kcl, :].rearrange("j p n -> p j n")
                else:
                    dst = wsc[key][:, :, kc, :].rearrange("j p n -> p j n")
                S.dma("sp", dst, stg[i][:, 0:dcols].rearrange("p (j n) -> p j n", n=cb), reads=[B_stg[i]], writes=[B_wsc[key]])

        precast("pw1", w_pw1, D, 2 * D, 512)
        precast("pw2", w_pw2, D, D, 512)
        for l in range(2):
            precast("wg%d" % l, wg[l], D, DFF, 256)
            precast("wu%d" % l, wu[l], D, DFF, 256)
            precast("wd%d" % l, wd[l], DFF, D, 512, down=True)
            if l == 0:
                precast("qkv", w_qkv, D, 3 * D, 512)
                precast("wo", w_o, D, D, 512)

    S.barrier()
    NSLOT = 6
    wslot = [sb("wslot%d" % i, [128, 8, 512], BF16) for i in range(NSLOT)]
    B_wslot = [Buf("wslot%d" % i) for i in range(NSLOT)]

    def blk_src(key):
        m = key[0]
        if m.startswith("wd"):
            _, jj, kg = key
            nk = 8 if kg < 2 else FC - 16
            return wsc[m][jj, kg, :, 0:nk, :], nk, 512
        if m.startswith("wg") or m.startswith("wu"):
            return wsc[m][key[1]], 8, 256
        return wsc[m][key[1]], 8, 512

    def ffn_blocks(l):
        out = []
        for jb in range(11):
            out.append(("wg%d" % l, jb))
            out.append(("wu%d" % l, jb))
        for jj in range(2):
            for kg in range(3):
                out.append(("wd%d" % l, jj, kg))
        return out

    L0_BLOCKS = [("pw1", 0), ("pw1", 2), ("pw1", 1), ("pw1", 3), ("pw2", 0), ("pw2", 1)] + ffn_blocks(0) + \
                [("qkv", j) for j in range(6)]
    L1_BLOCKS = [("wo", 0), ("wo", 1)] + ffn_blocks(1)
    plan = []
    for t in range(NT + 1):
        plan += L0_BLOCKS
    for t in range(NT + 1):
        plan += L1_BLOCKS
    wpos = [0, 0]

    def w_issue_upto(n):
        while wpos[1] < min(n, len(plan)):
            i = wpos[1]
            key = plan[i]
            src, nk, cb = blk_src(key)
            s = i % NSLOT
            S.dma("sp", wslot[s][:, 0:nk, 0:cb], src, reads=[B_wsc[key[0]]], writes=[B_wslot[s]])
            wpos[1] += 1

    def w_next(key):
        i = wpos[0]
        assert plan[i] == key, (i, plan[i], key)
        w_issue_upto(i + NSLOT - 2)
        wpos[0] += 1
        s = i % NSLOT
        return wslot[s], B_wslot[s]

    xtok = sb("xtok", [128, 4, D], F32)
    acc = sb("acc", [128, KC, TW], F32)
    xT = sb("xT", [128, KC, TW], F32)
    hT = [sb("hT%d" % i, [128, KC, TW], BF16) for i in range(2)]
    hid = sb("hid", [128, FC, TW], BF16)
    sq = sb("sq", [128, 2, TW], BF16)
    rs = sb("rs", [128, TW], F32)
    mean = sb("mean", [128, TW], F32)
    sg = [sb("sg%d" % i, [128, TW], F32) for i in range(2)]
    B_xtok = [Buf("xtok%d" % b) for b in range(4)]
    B_acc = [Buf("acc%d" % c) for c in range(KC)]
    B_xT = [Buf("xT%d" % c) for c in range(KC)]
    B_hT = [[Buf("hT%d_%d" % (i, c)) for c in range(KC)] for i in range(2)]
    B_uT = [Buf("uT%d" % c) for c in range(KC)]
    B_hid = [Buf("hid%d" % c) for c in range(FC)]
    B_sq = [Buf("sq%d" % i) for i in range(2)]
    B_rs = Buf("rs")
    B_mean = Buf("mean")
    B_sg = [Buf("sg%d" % i) for i in range(2)]
    B_vb16 = Buf("vb16")
    xsT = sb("xsT", [128, KC, NS], F32)
    B_xsT = Buf("xsT")
    B_cpre = Buf("cpre")
    qsT = sb("qsT", [128, KC, NS], BF16)
    ksT = sb("ksT", [128, KC, NS], BF16)
    vs32 = sb("vs32", [128, KC, NS], F32)
    B_qsT = Buf("qsT")
    B_ksT = Buf("ksT")
    B_vs32 = [Buf("vs32_%d" % c) for c in range(KC)]

    osT = sb("osT", [128, KC, NS], BF16)
    B_cst = Buf("cst")
    e0 = ExitStack()
    cst = sb("cst", [32, D], F32, e0)
    cpre = sb("cpre", [128, KC, NSS, CS], F32, e0)
    ks32 = sb("ks32", [128, KC, NS], F32, e0)
    B_ks32 = [Buf("ks32_%d" % c) for c in range(KC)]
    uT = sb("uT", [128, KC, CS + TW], F32, e0)
    vb16 = sb("vb16", [128, D], BF16, e0)

    def done():
        e0.close()
        es.close()
        S.finish("sp")
        S.close()
        return nc

    rr = [0]

    def next_ps(lo=0, hi=7):
        i = lo + rr[0] % (hi - lo)
        rr[0] += 1
        return i

    ev = [0]

    def evac_eng():
        ev[0] += 1
        return "act" if ev[0] % 2 else "dve"

    def load_tokmajor_T(src_rows_fn, W, dstT, B_dst, nblk, rows_per_blk):
        for b in range(nblk):
            S.dma("sp", xtok[0:rows_per_blk, b, :], src_rows_fn(b), writes=[B_xtok[b]])
        for c in range(KC):
            p = next_ps()
            for b in range(nblk):
                S.op("pe", lambda b=b, c=c, p=p: nc.tensor.transpose(
                    PS[p][:, b * 128:b * 128 + rows_per_blk], xtok[0:rows_per_blk, b, c * 128:(c + 1) * 128],
                    ident[0:rows_per_blk, 0:rows_per_blk]),
                    reads=[B_xtok[b], B_const], writes=[B_PS[p]], inc=(b == nblk - 1))
            e = evac_eng()
            if e == "act":
                S.op("act", lambda c=c, p=p: nc.scalar.copy(out=dstT[:, c, 0:W], in_=PS[p][:, 0:W]),
                     reads=[B_PS[p]], writes=[B_dst[c]])
            else:
                S.op("dve", lambda c=c, p=p: nc.vector.tensor_copy(out=dstT[:, c, 0:W], in_=PS[p][:, 0:W]),
                     reads=[B_PS[p]], writes=[B_dst[c]])

    def store_T_tokmajor(srcT, B_src, W, dst_rows_fn, nblk, rows_per_blk, extra_reads=()):
        for b in range(nblk):
            for c4 in range(2):
                p = next_ps()
                for cc in range(4):
                    c = c4 * 4 + cc
                    S.op("pe", lambda b=b, c=c, cc=cc, p=p: nc.tensor.transpose(
                        PS[p][0:rows_per_blk, cc * 128:(cc + 1) * 128], srcT[:, c, b * 128:b * 128 + rows_per_blk], ident[:, :]),
                        reads=[B_src[c], B_const], writes=[B_PS[p]], inc=(cc == 3))
                e = evac_eng()
                if e == "act":
                    S.op("act", lambda b=b, c4=c4, p=p: nc.scalar.copy(out=xtok[0:rows_per_blk, b, c4 * 512:(c4 + 1) * 512],
                                                                       in_=PS[p][0:rows_per_blk, :]),
                         reads=[B_PS[p]], writes=[B_xtok[b]])
                else:
                    S.op("dve", lambda b=b, c4=c4, p=p: nc.vector.tensor_copy(out=xtok[0:rows_per_blk, b, c4 * 512:(c4 + 1) * 512],
                                                                              in_=PS[p][0:rows_per_blk, :]),
                         reads=[B_PS[p]], writes=[B_xtok[b]])
            S.dma("pool", dst_rows_fn(b), xtok[0:rows_per_blk, b, :], reads=[B_xtok[b]])

    def rmsnorm(srcT, B_src, W, gname, grow, dstT, B_dst):
        p = next_ps()
        for c in range(KC):
            i = c % 2
            S.op("act", lambda c=c, i=i: nc.scalar.activation(out=sq[:, i, 0:W], in_=srcT[:, c, 0:W], func=AF.Square),
                 reads=[B_src[c]], writes=[B_sq[i]])
            S.op("pe", lambda c=c, i=i, p=p: nc.tensor.matmul(PS[p][:, 0:W], lhsT=onesD[:], rhs=sq[:, i, 0:W],
                                                              start=(c == 0), stop=(c == KC - 1)),
                 reads=[B_sq[i], B_const], writes=[B_PS[p]])
        S.op("act", lambda p=p: nc.scalar.activation(out=rs[:, 0:W], in_=PS[p][:, 0:W], func=AF.Sqrt, bias=RMS_EPS, scale=1.0),
             reads=[B_PS[p]], writes=[B_rs])
        S.op("dve", lambda: nc.vector.reciprocal(out=rs[:, 0:W], in_=rs[:, 0:W]), reads=[B_rs], writes=[B_rs])
        for c in range(KC):
            S.op("dve", lambda c=c: nc.vector.scalar_tensor_tensor(out=dstT[:, c, 0:W], in0=srcT[:, c, 0:W],
                                                                   scalar=vcol(gname, grow, c), in1=rs[:, 0:W],
                                                                   op0=ALU.mult, op1=ALU.mult),
                 reads=[B_src[c], B_rs, B_vT], writes=[B_dst[c]])

    def mm_group(p, W, wt, B_w, col0, inT, B_in, nk, kc0=0, start=True, stop=True, extra_w=()):
        for k in range(nk):
            last = (k == nk - 1)
            S.op("pe", lambda k=k: nc.tensor.matmul(PS[p][:, 0:W], lhsT=wt[:, k, col0:col0 + 128], rhs=inT[:, kc0 + k, 0:W],
                                                    start=(start and k == 0), stop=(stop and last)),
                 reads=[B_w, B_in[kc0 + k]], writes=[B_PS[p]], inc=last)

    def ffn(l, W, inT, B_in, resT, B_res):
        for jb in range(11):
            wgt, Bg = w_next(("wg%d" % l, jb))
            wut, Bu = w_next(("wu%d" % l, jb))
            for half in range(2):
                fc = jb * 2 + half
                pg = next_ps()
                mm_group(pg, W, wgt, Bg, half * 128, inT, B_in, KC)
                pu = next_ps()
                mm_group(pu, W, wut, Bu, half * 128, inT, B_in, KC)
                i = fc % 2
                S.op("act", lambda pg=pg, i=i: nc.scalar.activation(out=sg[i][:, 0:W], in_=PS[pg][:, 0:W], func=AF.Silu),
                     reads=[B_PS[pg]], writes=[B_sg[i]])
                S.op("dve", lambda pu=pu, i=i, fc=fc: nc.vector.tensor_tensor(out=hid[:, fc, 0:W], in0=PS[pu][:, 0:W],
                                                                              in1=sg[i][:, 0:W], op=ALU.mult),
                     reads=[B_PS[pu], B_sg[i]], writes=[B_hid[fc]])
        for jj in range(2):
            wts = [w_next(("wd%d" % l, jj, kg)) for kg in range(3)]
            for oc4 in range(4):
                oc = jj * 4 + oc4
                p = next_ps()
                for kg in range(3):
                    nk = 8 if kg < 2 else FC - 16
                    mm_group(p, W, wts[kg][0], wts[kg][1], oc4 * 128, hid, B_hid, nk, kc0=kg * 8,
                             start=(kg == 0), stop=(kg == 2))
                S.op("dve", lambda oc=oc, p=p: nc.vector.tensor_tensor(out=resT[:, oc, 0:W], in0=resT[:, oc, 0:W],
                                                                       in1=PS[p][:, 0:W], op=ALU.add),
                     reads=[B_PS[p], B_res[oc]], writes=[B_res[oc]])

    def conv_mixer(W, curT, B_cur, hin, B_hin, hout, B_hout, segs):
        for half in range(2):
            wa, Ba = w_next(("pw1", half))
            wgt, Bg = w_next(("pw1", 2 + half))
            for oc4 in range(4):
                oc = half * 4 + oc4
                pa = next_ps()
                mm_group(pa, W, wa, Ba, oc4 * 128, hin, B_hin, KC)
                pg = next_ps()
                mm_group(pg, W, wgt, Bg, oc4 * 128, hin, B_hin, KC)
                i = oc % 2
                S.op("act", lambda pg=pg, i=i, oc=oc: nc.scalar.activation(out=sg[i][:, 0:W], in_=PS[pg][:, 0:W], func=AF.Sigmoid,
                                                                           bias=vcol("b_pw1", 1, oc), scale=1.0),
                     reads=[B_PS[pg], B_vT], writes=[B_sg[i]])
                S.op("dve", lambda pa=pa, i=i, oc=oc: nc.vector.scalar_tensor_tensor(
                    out=uT[:, oc, CS:CS + W], in0=PS[pa][:, 0:W], scalar=vcol("b_pw1", 0, oc), in1=sg[i][:, 0:W],
                    op0=ALU.add, op1=ALU.mult), reads=[B_PS[pa], B_sg[i], B_vT], writes=[B_uT[oc]])


    def conv_taps(c, ncols, src_fn, dst_ap, reads, writes):
        S.op("dve", lambda: nc.vector.tensor_scalar(out=dst_ap, in0=src_fn(0), scalar1=vcol("w_dw", 0, c), scalar2=vcol("b_dw", 0, c),
                                                    op0=ALU.mult, op1=ALU.add), reads=reads + [B_vT], writes=writes)
        for k in range(1, CW):
            S.op("dve", lambda k=k: nc.vector.scalar_tensor_tensor(out=dst_ap, in0=src_fn(k), scalar=vcol("w_dw", k, c), in1=dst_ap,
                                                                   op0=ALU.mult, op1=ALU.add), reads=reads + [B_vT], writes=writes)

    def ln_silu(W, srcT, B_src, dstT, B_dst):
        pm = next_ps()
        pq = next_ps()
        for c in range(KC):
            i = c % 2
            S.op("act", lambda c=c, i=i: nc.scalar.activation(out=sq[:, i, 0:W], in_=srcT[:, c, 0:W], func=AF.Square),
                 reads=[B_src[c]], writes=[B_sq[i]])
            S.op("pe", lambda c=c, i=i: nc.tensor.matmul(PS[pq][:, 0:W], lhsT=onesD[:], rhs=sq[:, i, 0:W],
                                                         start=(c == 0), stop=(c == KC - 1)),
                 reads=[B_sq[i], B_const], writes=[B_PS[pq]])
        for c in range(KC):
            i = c % 2
            S.op("dve", lambda c=c, i=i: nc.vector.tensor_copy(out=hT[1][:, i, 0:W], in_=srcT[:, c, 0:W]),
                 reads=[B_src[c]], writes=[B_hT[1][i]])
            S.op("pe", lambda c=c, i=i: nc.tensor.matmul(PS[pm][:, 0:W], lhsT=onesD[:], rhs=hT[1][:, i, 0:W],
                                                         start=(c == 0), stop=(c == KC - 1)),
                 reads=[B_hT[1][i], B_const], writes=[B_PS[pm]])
        S.op("dve", lambda: nc.vector.tensor_copy(out=mean[:, 0:W], in_=PS[pm][:, 0:W]), reads=[B_PS[pm]], writes=[B_mean])
        S.op("dve", lambda: nc.vector.tensor_tensor(out=rs[:, 0:W], in0=mean[:, 0:W], in1=mean[:, 0:W], op=ALU.mult),
             reads=[B_mean], writes=[B_rs])
        S.op("dve", lambda: nc.vector.tensor_tensor(out=rs[:, 0:W], in0=PS[pq][:, 0:W], in1=rs[:, 0:W], op=ALU.subtract),
             reads=[B_PS[pq], B_rs], writes=[B_rs])
        S.op("dve", lambda: nc.vector.tensor_scalar(out=rs[:, 0:W], in0=rs[:, 0:W], scalar1=0.0, scalar2=None, op0=ALU.max),
             reads=[B_rs], writes=[B_rs])
        S.op("act", lambda: nc.scalar.activation(out=rs[:, 0:W], in_=rs[:, 0:W], func=AF.Sqrt, bias=LN_EPS, scale=1.0),
             reads=[B_rs], writes=[B_rs])
        S.op("dve", lambda: nc.vector.reciprocal(out=rs[:, 0:W], in_=rs[:, 0:W]), reads=[B_rs], writes=[B_rs])
        for c in range(KC):
            S.op("dve", lambda c=c: nc.vector.tensor_tensor(out=srcT[:, c, 0:W], in0=srcT[:, c, 0:W], in1=mean[:, 0:W], op=ALU.subtract),
                 reads=[B_src[c], B_mean], writes=[B_src[c]])
            S.op("dve", lambda c=c: nc.vector.tensor_tensor(out=srcT[:, c, 0:W], in0=srcT[:, c, 0:W], in1=rs[:, 0:W], op=ALU.mult),
                 reads=[B_src[c], B_rs], writes=[B_src[c]])
            S.op("act", lambda c=c: nc.scalar.activation(out=dstT[:, c, 0:W], in_=srcT[:, c, 0:W], func=AF.Silu,
                                                         bias=vcol("ln_b", 0, c), scale=vcol("ln_g", 0, c)),
                 reads=[B_src[c], B_vT], writes=[B_dst[c]])

    def pw2_residual(W, inT, B_in, resT, B_res):
        for jj in range(2):
            wt, Bw = w_next(("pw2", jj))
            for oc4 in range(4):
                oc = jj * 4 + oc4
                p = next_ps()
                mm_group(p, W, wt, Bw, oc4 * 128, inT, B_in, KC)
                S.op("dve", lambda oc=oc, p=p: nc.vector.scalar_tensor_tensor(
                    out=resT[:, oc, 0:W], in0=PS[p][:, 0:W], scalar=vcol("b_pw2", 0, oc), in1=resT[:, oc, 0:W],
                    op0=ALU.add, op1=ALU.add), reads=[B_PS[p], B_res[oc], B_vT], writes=[B_res[oc]])

    def qkv_tile(W, hin, B_hin, tok0, nblk, rows, qT_dst, kT_dst, B_qdst, B_kdst, k_out, v_out, v_bf_dst, B_vbf):
        for which, dst_fn, Bd in (("q", qT_dst, B_qdst), ("k", kT_dst, B_kdst)):
            for jj in range(2):
                wt, Bw = w_next(("qkv", (0 if which == "q" else 2) + jj))
                if which == "k":
                    kw[jj] = (wt, Bw)
                for oc4 in range(4):
                    oc = jj * 4 + oc4
                    p = next_ps()
                    mm_group(p, W, wt, Bw, oc4 * 128, hin, B_hin, KC)
                    i = oc % 2
                    dst_fn(oc, p, i)
        accv = acc[:].rearrange("p c t -> p (c t)")
        for which in ("k", "v"):
            wl = kw if which == "k" else [w_next(("qkv", 4 + jj)) for jj in range(2)]
            for b in range(nblk):
                for jj in range(2):
                    wt, Bw = wl[jj]
                    p = next_ps()
                    for k in range(KC):
                        S.op("pe", lambda k=k, wt=wt, p=p, b=b: nc.tensor.matmul(
                            PS[p][0:rows, :], lhsT=hin[:, k, b * 128:b * 128 + rows], rhs=wt[:, k, :],
                            start=(k == 0), stop=(k == KC - 1)), reads=[Bw, B_hin[k]], writes=[B_PS[p]], inc=(k == KC - 1))
                    if which == "k":
                        S.op("act", lambda p=p, b=b, jj=jj: nc.scalar.copy(out=xtok[0:rows, b, jj * 512:(jj + 1) * 512], in_=PS[p][0:rows, :]),
                             reads=[B_PS[p]], writes=[B_xtok[b]])
                    else:
                        S.op("act", lambda p=p, b=b, jj=jj: nc.scalar.copy(
                            out=accv[0:rows, b * D + jj * 512:b * D + (jj + 1) * 512], in_=PS[p][0:rows, :]),
                            reads=[B_PS[p]], writes=[B_acc[2 * b], B_acc[2 * b + 1]])
        for b in range(nblk):
            S.dma("pool", k_out(b), xtok[0:rows, b, :], reads=[B_xtok[b]])
            S.dma("pool", v_out(b), accv[0:rows, b * D:(b + 1) * D], reads=[B_acc[2 * b], B_acc[2 * b + 1]])
            v_bf_dst(b, accv)

    kw = [None, None]

    def tokmajor_out(srcT_fn, B_src, ncols, dst_ap):
        for c4 in range(2):
            p = next_ps()
            for cc in range(4):
                c = c4 * 4 + cc
                S.op("pe", lambda c=c, cc=cc, p=p: nc.tensor.transpose(PS[p][0:ncols, cc * 128:(cc + 1) * 128],
                                                                       srcT_fn(c), ident[:, :]),
                     reads=[B_src[c], B_const], writes=[B_PS[p]], inc=(cc == 3))
            S.op("dve", lambda c4=c4, p=p: nc.vector.tensor_copy(out=cst[0:ncols, c4 * 512:(c4 + 1) * 512], in_=PS[p][0:ncols, :]),
                 reads=[B_PS[p]], writes=[B_cst])
        S.dma("pool", dst_ap, cst[0:ncols, :], reads=[B_cst])

    def layer0_tail(W, c0, is_sample):
        ln_silu(W, acc, B_acc, hT[0], B_hT[0])
        pw2_residual(W, hT[0], B_hT[0], xT, B_xT)
        rmsnorm(xT, B_xT, W, "ffn_g", 0, hT[1], B_hT[1])
        ffn(0, W, hT[1], B_hT[1], xT, B_xT)

    for t in range(NT):
        W = TW
        c0 = t * TW
        load_tokmajor_T(lambda b, c0=c0: xp[c0 + b * 128:c0 + (b + 1) * 128, :], W, xT, B_xT, 4, 128)
        if stage == 0:
            return done()
        rmsnorm(xT, B_xT, W, "mix_g", 0, hT[0], B_hT[0])
        if stage == 0.1:
            return done()
        for c in range(KC):
            if t == 0:
                S.op("dve", lambda c=c: nc.vector.memset(uT[:, c, 0:CS], 0.0), writes=[B_uT[c]])
            else:
                S.op("dve", lambda c=c: nc.vector.tensor_copy(out=uT[:, c, 0:CS], in_=uT[:, c, TW:TW + CS]),
                     reads=[B_uT[c]], writes=[B_uT[c]])
        conv_mixer(W, xT, B_xT, hT[0], B_hT[0], None, None, None)
        if stage == 0.2:
            return done()
        if t == NT - 1:
            tokmajor_out(lambda c: uT[:, c, TW:TW + CS], B_uT, CS, csp[:, :])
        if stage == 0.25:
            return done()
        for c in range(KC):
            conv_taps(c, W, lambda k, c=c: uT[:, c, k:k + W], acc[:, c, 0:W], [B_uT[c]], [B_acc[c]])
        if stage == 0.3:
            return done()
        layer0_tail(W, c0, False)
        if stage == 0.4:
            return done()
        for c in range(KC):
            S.dma("pool", x1T[c, :, c0:c0 + W], xT[:, c, 0:W], reads=[B_xT[c]], writes=[B_x1T[c]])
        rmsnorm(xT, B_xT, W, "mix_g", 1, hT[0], B_hT[0])

        def q_dst(oc, p, i, c0=c0, W=W):
            S.op("act", lambda: nc.scalar.copy(out=hT[1][:, oc, 0:W], in_=PS[p][:, 0:W]), reads=[B_PS[p]], writes=[B_hT[1][oc]])
            S.dma("pool", qTs[oc, :, c0:c0 + W], hT[1][:, oc, 0:W], reads=[B_hT[1][oc]], writes=[B_qTs[oc]])

        def k_dst(oc, p, i, c0=c0, W=W):
            S.op("dve", lambda: nc.vector.tensor_copy(out=hid[:, oc, 0:W], in_=PS[p][:, 0:W]), reads=[B_PS[p]], writes=[B_hid[oc]])
            S.dma("pool", kTs[oc, :, c0:c0 + W], hid[:, oc, 0:W], reads=[B_hid[oc]], writes=[B_kTs[oc]])

        def v_bf(b, accv, c0=c0):
            S.op("dve", lambda: nc.vector.tensor_copy(out=vb16[:, :], in_=accv[:, b * D:(b + 1) * D]),
                 reads=[B_acc[2 * b], B_acc[2 * b + 1]], writes=[B_vb16])
            S.dma("pool", vtok[c0 + b * 128:c0 + (b + 1) * 128, :], vb16[:, :], reads=[B_vb16], writes=[B_vtok])

        qkv_tile(W, hT[0], B_hT[0], c0, 4, 128, q_dst, k_dst, B_hT[1], B_hid,
                 lambda b, c0=c0: kp[c0 + b * 128:c0 + (b + 1) * 128, :],
                 lambda b, c0=c0: vp[c0 + b * 128:c0 + (b + 1) * 128, :], v_bf, B_vb16)

    if stage == 0.5:
        return done()
    if sample:
        W = NS
        load_tokmajor_T(lambda b: xs[:, :], W, xT, B_xT, 1, NS)
        rmsnorm(xT, B_xT, W, "mix_g", 0, hT[0], B_hT[0])
        conv_mixer(W, xT, B_xT, hT[0], B_hT[0], None, None, None)
        for i in range(NSS):
            S.dma("sp", xtok[0:CS, i, :], cconv[i * CS:(i + 1) * CS, :], writes=[B_xtok[i]])
        for c in range(KC):
            p = next_ps()
            for i in range(NSS):
                S.op("pe", lambda c=c, i=i, p=p: nc.tensor.transpose(PS[p][:, i * 32:i * 32 + CS], xtok[0:CS, i, c * 128:(c + 1) * 128],
                                                                     ident[0:CS, 0:CS]),
                     reads=[B_xtok[i], B_const], writes=[B_PS[p]], inc=(i == NSS - 1))
            S.op("dve", lambda c=c, p=p: nc.vector.tensor_copy(
                out=cpre[:, c, :, :], in_=PS[p][:, 0:NSS * 32].rearrange("p (i t) -> p i t", t=32)[:, :, 0:CS]),
                reads=[B_PS[p]], writes=[B_cpre])
        if stage == 0.6:
            return done()
        XO = 64
        for c in range(KC):
            for i in range(NSS):
                o = XO + i * 40
                S.op("dve", lambda c=c, i=i, o=o: nc.vector.tensor_copy(out=uT[:, c, o:o + CS], in_=cpre[:, c, i, :]),
                     reads=[B_cpre, B_uT[c]], writes=[B_uT[c]])
                S.op("dve", lambda c=c, i=i, o=o: nc.vector.tensor_copy(out=uT[:, c, o + CS:o + CS + DSQ],
                                                                        in_=uT[:, c, CS + i * DSQ:CS + (i + 1) * DSQ]),
                     reads=[B_uT[c]], writes=[B_uT[c]])
        for i in range(NSS):
            o = XO + i * 40
            tokmajor_out(lambda c, o=o: uT[:, c, o + DSQ:o + DSQ + CS], B_uT, CS, css[i * CS:(i + 1) * CS, :])
        if stage == 0.7:
            return done()
        for c in range(KC):
            for i in range(NSS):
                o = XO + i * 40
                conv_taps(c, DSQ, lambda k, c=c, o=o: uT[:, c, o + k:o + k + DSQ], acc[:, c, i * DSQ:(i + 1) * DSQ], [B_uT[c]], [B_acc[c]])
        if stage == 0.75:
            return done()
        layer0_tail(W, 0, True)
        if stage == 0.8:
            return done()
        SAMPLE_QKV = True
        if SAMPLE_QKV:
            S.op("dve", lambda: nc.vector.tensor_copy(out=xsT[:, :, :], in_=xT[:, :, 0:NS]), reads=B_xT, writes=[B_xsT])
            rmsnorm(xT, B_xT, W, "mix_g", 1, hT[0], B_hT[0])

            for which, base in (("q", 0), ("k", 2), ("v", 4)):
                for jj in range(2):
                    wt, Bw = w_next(("qkv", base + jj))
                    for oc4 in range(4):
                        oc = jj * 4 + oc4
                        p = next_ps()
                        mm_group(p, NS, wt, Bw, oc4 * 128, hT[0], B_hT[0], KC)
                        if which == "q":
                            S.op("act", lambda oc=oc, p=p: nc.scalar.copy(out=qsT[:, oc, :], in_=PS[p][:, 0:NS]),
                                 reads=[B_PS[p]], writes=[B_qsT])
                        elif which == "k":
                            S.op("act", lambda oc=oc, p=p: nc.scalar.copy(out=ks32[:, oc, :], in_=PS[p][:, 0:NS]),
                                 reads=[B_PS[p]], writes=[B_ks32[oc]])
                            S.op("dve", lambda oc=oc, p=p: nc.vector.tensor_copy(out=ksT[:, oc, :], in_=PS[p][:, 0:NS]),
                                 reads=[B_PS[p]], writes=[B_ksT])
                        else:
                            S.op("act", lambda oc=oc, p=p: nc.scalar.copy(out=vs32[:, oc, :], in_=PS[p][:, 0:NS]),
                                 reads=[B_PS[p]], writes=[B_vs32[oc]])
            tokmajor_out(lambda c: ks32[:, c, :], B_ks32, NS, ks[:, :])
            tokmajor_out(lambda c: vs32[:, c, :], B_vs32, NS, vs[:, :])
        else:
            wpos[0] += 6
            w_issue_upto(wpos[0])
    if not sample:
        wpos[0] += len(L0_BLOCKS)
        w_issue_upto(wpos[0])
    e0.close()
    S.barrier()

    sa = ExitStack()
    KTc = sb("KTc", [128, SEQ], BF16, sa)
    QTc = sb("QTc", [128, SEQ], BF16, sa)
    Vc = sb("Vc", [128, NBLK, 128], BF16, sa)
    OTc = sb("OTc", [128, SEQ], BF16, sa)
    Et = [sb("Et%d" % i, [128, TW], F32, sa) for i in range(2)]
    Lt = [sb("Lt%d" % i, [128, TW], BF16, sa) for i in range(2)]
    At = [sb("At%d" % i, [128, TW], BF16, sa) for i in range(2)]
    Lb = [sb("Lb%d" % i, [128, TW], BF16, sa) for i in range(2)]
    Lacc = sb("Lacc", [128, TW], F32, sa)
    B_KTc, B_QTc, B_Vc, B_OTc, B_Lacc = Buf("KTc"), Buf("QTc"), Buf("Vc"), Buf("OTc"), Buf("Lacc")
    B_Et = [Buf("Et%d" % i) for i in range(2)]
    B_Lt = [Buf("Lt%d" % i) for i in range(2)]
    B_At = [Buf("At%d" % i) for i in range(2)]
    B_Lb = [Buf("Lb%d" % i) for i in range(2)]
    stepc = [0]
    unitc = [0]

    def sb_step(z, bi, a0, WW, diag, has_carry, carry_lo, kT_ap, q_ap, bias_ap, prev_bi, need_acc):
        cols = slice(a0, WW)
        S.op("pe", lambda: nc.tensor.matmul(PS[z][:, cols], lhsT=kT_ap, rhs=q_ap, start=True, stop=False),
             reads=[B_KTc, B_QTc], writes=[B_PS[z]])
        S.op("act", lambda: nc.scalar.activation(out=Et[bi][:, cols], in_=PS[z][:, cols], func=AF.Exp, bias=bias_ap, scale=0.125),
             reads=[B_PS[z], B_biash], writes=[B_Et[bi]])
        S.op("act", lambda: nc.scalar.activation(out=Lt[bi][:, cols], in_=Et[bi][:, cols], func=AF.Ln, bias=1.0, scale=1.0),
             reads=[B_Et[bi]], writes=[B_Lt[bi]])
        if diag:
            S.op("dve", lambda: nc.vector.tensor_tensor(out=Lt[bi][:, a0:a0 + 128], in0=Lt[bi][:, a0:a0 + 128], in1=trim[:, :], op=ALU.mult),
                 reads=[B_Lt[bi], B_const], writes=[B_Lt[bi]])
        if not has_carry:
            S.op("pe", lambda: nc.tensor.matmul(PS[z][:, cols], lhsT=tri8[:, :], rhs=Lt[bi][:, cols], start=False, stop=True),
                 reads=[B_Lt[bi], B_const], writes=[B_PS[z]])
        else:
            if carry_lo > a0:
                S.op("pe", lambda: nc.tensor.matmul(PS[z][:, a0:carry_lo], lhsT=tri8[:, :], rhs=Lt[bi][:, a0:carry_lo], start=False, stop=True),
                     reads=[B_Lt[bi], B_const], writes=[B_PS[z]])
            cc = slice(carry_lo, WW)
            S.op("pe", lambda: nc.tensor.matmul(PS[z][:, cc], lhsT=tri8[:, :], rhs=Lt[bi][:, cc], start=False, stop=False),
                 reads=[B_Lt[bi], B_const], writes=[B_PS[z]])
            S.op("pe", lambda: nc.tensor.matmul(PS[z][:, cc], lhsT=one8[:, :], rhs=Lb[prev_bi][:, cc], start=False, stop=True),
                 reads=[B_Lb[prev_bi], B_const], writes=[B_PS[z]])
        if need_acc:
            S.op("dve", lambda: nc.vector.tensor_tensor(out=Lb[bi][:, cols], in0=Lacc[:, cols], in1=Lt[bi][:, cols], op=ALU.add),
                 reads=[B_Lacc, B_Lt[bi]], writes=[B_Lb[bi]])
            S.op("dve", lambda: nc.vector.tensor_tensor(out=Lacc[:, cols], in0=Lacc[:, cols], in1=Lt[bi][:, cols], op=ALU.add),
                 reads=[B_Lacc, B_Lt[bi]], writes=[B_Lacc])
        S.op("act", lambda: nc.scalar.activation(out=At[bi][:, cols], in_=PS[z][:, cols], func=AF.Exp, bias=bias_ap, scale=0.125),
             reads=[B_PS[z], B_biash], writes=[B_At[bi]])
        if diag:
            S.op("dve", lambda: nc.vector.tensor_tensor(out=At[bi][:, a0:a0 + 128], in0=At[bi][:, a0:a0 + 128], in1=trim[:, :], op=ALU.mult),
                 reads=[B_At[bi], B_const], writes=[B_At[bi]])

    if stage >= 2:
        for c in range(KC):
            S.dma("sp", KTc[:, :], kTs[c], reads=[B_kTs[c]], writes=[B_KTc])
            S.dma("sp", QTc[:, :], qTs[c], reads=[B_qTs[c]], writes=[B_QTc])
            for b0 in range(0, NBLK, 4):
                S.dma("sp", Vc[:, b0:b0 + 4, :],
                      vtok[b0 * 128:(b0 + 4) * 128, c * 128:(c + 1) * 128].rearrange("(b p) f -> p b f", p=128),
                      reads=[B_vtok], writes=[B_Vc])
            for hp in range(2):
                h = 2 * c + hp
                pr = slice(hp * 64, hp * 64 + 64)
                for qi in range(NT):
                    oa = 2 + unitc[0] % 2
                    unitc[0] += 1
                    S.op("dve", lambda: nc.vector.memset(Lacc[:, :], 0.0), reads=[B_Lacc], writes=[B_Lacc])
                    jmax = 4 * qi + 3
                    prev_bi = 0
                    for j in range(jmax, -1, -1):
                        bq = max(0, j - 4 * qi)
                        a0 = bq * 128
                        diag = j >= 4 * qi
                        first = j == jmax
                        last = j == 0
                        z = stepc[0] % 2
                        bi = z
                        stepc[0] += 1
                        carry_lo = a0 + 128 if diag else a0
                        has_carry = (not first) and carry_lo < TW
                        sb_step(z, bi, a0, TW, diag, has_carry, carry_lo,
                                KTc[pr, j * 128:(j + 1) * 128], QTc[pr, qi * TW + a0:(qi + 1) * TW],
                                bias_h[:, h:h + 1], prev_bi, not last)
                        S.op("pe", lambda j=j, a0=a0, bi=bi, oa=oa, first=first, last=last: nc.tensor.matmul(
                            PS[oa][:, a0:TW], lhsT=Vc[:, j, :], rhs=At[bi][:, a0:TW], start=first, stop=last, skip_group_check=True),
                            reads=[B_Vc, B_At[bi]], writes=[B_PS[oa]])
                        prev_bi = bi
                    S.op("dve", lambda oa=oa, pr=pr, qi=qi: nc.vector.tensor_copy(out=OTc[pr, qi * TW:(qi + 1) * TW], in_=PS[oa][pr, :]),
                         reads=[B_PS[oa]], writes=[B_OTc])
            S.dma("pool", oTs[c], OTc[:, :], reads=[B_OTc], writes=[B_oTs[c]])

    sa.close()
    S.barrier()

    def layer1_tail(W, oT_in, B_oin, out_fn, nblk, rows):
        for jj in range(2):
            wt, Bw = w_next(("wo", jj))
            for oc4 in range(4):
                oc = jj * 4 + oc4
                p = next_ps()
                mm_group(p, W, wt, Bw, oc4 * 128, oT_in, B_oin, KC)
                S.op("dve", lambda oc=oc, p=p: nc.vector.tensor_tensor(out=xT[:, oc, 0:W], in0=xT[:, oc, 0:W], in1=PS[p][:, 0:W], op=ALU.add),
                     reads=[B_PS[p], B_xT[oc]], writes=[B_xT[oc]])
        rmsnorm(xT, B_xT, W, "ffn_g", 1, hT[1], B_hT[1])
        ffn(1, W, hT[1], B_hT[1], xT, B_xT)
        rmsnorm(xT, B_xT, W, "fin_g", 0, acc, B_acc)
        store_T_tokmajor(acc, B_acc, W, out_fn, nblk, rows)

    if stage >= 3:
        for t in range(NT):
            c0 = t * TW
            for c in range(KC):
                S.dma("sp", xT[:, c, :], x1T[c, :, c0:c0 + TW], reads=[B_x1T[c]], writes=[B_xT[c]])
                S.dma("sp", hT[0][:, c, :], oTs[c, :, c0:c0 + TW], reads=[B_oTs[c]], writes=[B_hT[0][c]])
            layer1_tail(TW, hT[0], B_hT[0], lambda b, c0=c0: yp[c0 + b * 128:c0 + (b + 1) * 128, :], 4, 128)


    B_osT = Buf("osT")
    if sample and stage >= 4:
        ss = ExitStack()
        Kpg = [sb("Kpg%d" % i, [128, D], F32, ss) for i in range(2)]
        Vpg = [sb("Vpg%d" % i, [128, D], F32, ss) for i in range(2)]
        KpT = sb("KpT", [128, KC, 128], BF16, ss)
        Vpb = sb("Vpb", [128, D], BF16, ss)
        vsn = sb("vsn", [DSQ, NSS, D], BF16, ss)
        Qbd = sb("Qbd", [128, KC, NSS, 16], BF16, ss)
        ptb = sb("ptb", [128, NSS * NPG], I32, ss)
        ptf = sb("ptf", [128, NSS * NPG], F32, ss)
        iot = sb("iot", [128, 1], F32, ss)
        idx = sb("idx", [128, NSS * NPG], I32, ss)
        onef = sb("onef", [1, 128], F32, ss)
        bf32 = sb("bf32", [1, 128], F32, ss)
        lo32 = sb("lo32", [1, 128], F32, ss)
        bhi = sb("bhi", [1, 128], BF16, ss)
        blo = sb("blo", [1, 128], BF16, ss)
        one1 = sb("one1", [1, 128], BF16, ss)
        mask8 = sb("mask8", [DSQ, 128], BF16, ss)
        Es = [sb("Es%d" % i, [128, 128], F32, ss) for i in range(2)]
        Ls = [sb("Ls%d" % i, [128, 128], BF16, ss) for i in range(2)]
        As = [sb("As%d" % i, [128, 128], BF16, ss) for i in range(2)]
        Lbs = [sb("Lbs%d" % i, [128, 128], BF16, ss) for i in range(2)]
        Lac = sb("Lac", [128, 128], F32, ss)
        B_Kpg = [Buf("Kpg%d" % i) for i in range(2)]
        B_Vpg = [Buf("Vpg%d" % i) for i in range(2)]
        B_KpT, B_Vpb, B_vsn, B_Qbd, B_idx = Buf("KpT"), Buf("Vpb"), Buf("vsn"), Buf("Qbd"), Buf("idx")
        B_ptb, B_ptf, B_iot, B_bias = Buf("ptb"), Buf("ptf"), Buf("iot"), Buf("sbias")
        B_Es = [Buf("Es%d" % i) for i in range(2)]
        B_Ls = [Buf("Ls%d" % i) for i in range(2)]
        B_As = [Buf("As%d" % i) for i in range(2)]
        B_Lbs = [Buf("Lbs%d" % i) for i in range(2)]
        B_Lac = Buf("Lac")
        S.op("pool", lambda: g.iota(iot[:], pattern=[[0, 1]], base=0, channel_multiplier=1, allow_small_or_imprecise_dtypes=True),
             writes=[B_iot])
        S.dma("sp", ptb[:], pt.rearrange("i j -> (i j)").partition_broadcast(128), writes=[B_ptb])
        S.op("dve", lambda: nc.vector.tensor_copy(out=ptf[:], in_=ptb[:]), reads=[B_ptb], writes=[B_ptf])
        S.op("dve", lambda: nc.vector.tensor_scalar(out=ptf[:], in0=ptf[:], scalar1=128.0, scalar2=iot[:, 0:1], op0=ALU.mult, op1=ALU.add),
             reads=[B_ptf, B_iot], writes=[B_ptf])
        S.op("dve", lambda: nc.vector.tensor_copy(out=idx[:], in_=ptf[:]), reads=[B_ptf], writes=[B_idx])
        S.op("dve", lambda: nc.vector.memset(onef[:], 1.0), writes=[B_bias])
        S.op("dve", lambda: nc.vector.memset(one1[:], 1.0), reads=[B_bias], writes=[B_bias])
        for h in range(NH):
            S.op("dve", lambda h=h: nc.vector.tensor_scalar(out=bf32[0:1, h * 8:(h + 1) * 8], in0=onef[0:1, 0:8],
                                                            scalar1=bias_h[0:1, h:h + 1], scalar2=8.0, op0=ALU.mult, op1=ALU.mult),
                 reads=[B_biash, B_bias], writes=[B_bias])
        S.op("dve", lambda: nc.vector.tensor_copy(out=bhi[:], in_=bf32[:]), reads=[B_bias], writes=[B_bias])
        S.op("dve", lambda: nc.vector.tensor_tensor(out=lo32[:], in0=bf32[:], in1=bhi[:], op=ALU.subtract), reads=[B_bias], writes=[B_bias])
        S.op("dve", lambda: nc.vector.tensor_copy(out=blo[:], in_=lo32[:]), reads=[B_bias], writes=[B_bias])
        for h in range(NH):
            S.op("dve", lambda h=h: nc.vector.tensor_copy(out=mask8[:, h * 8:(h + 1) * 8], in_=trim[0:DSQ, 0:DSQ]),
                 reads=[B_const, B_bias], writes=[B_bias])
        S.op("dve", lambda: nc.vector.memset(Qbd[:], 0.0), writes=[B_Qbd])
        qv = qsT[:].rearrange("p k (i t) -> p k i t", t=DSQ)
        S.op("dve", lambda: nc.vector.tensor_copy(out=Qbd[0:64, :, :, 0:8], in_=qv[0:64, :, :, :]), reads=[B_qsT, B_Qbd], writes=[B_Qbd])
        S.op("dve", lambda: nc.vector.tensor_copy(out=Qbd[64:128, :, :, 8:16], in_=qv[64:128, :, :, :]), reads=[B_qsT, B_Qbd], writes=[B_Qbd])
        for i in range(NSS):
            for c4 in range(2):
                p = next_ps(4, 7)
                for cc in range(4):
                    c = c4 * 4 + cc
                    S.op("pe", lambda c=c, cc=cc, p=p, i=i: nc.tensor.transpose(PS[p][0:DSQ, cc * 128:(cc + 1) * 128],
                                                                                vs32[:, c, i * DSQ:(i + 1) * DSQ], ident[:, :]),
                         reads=[B_vs32[c], B_const], writes=[B_PS[p]], inc=(cc == 3))
                S.op("dve", lambda c4=c4, p=p, i=i: nc.vector.tensor_copy(out=vsn[0:DSQ, i, c4 * 512:(c4 + 1) * 512], in_=PS[p][0:DSQ, :]),
                     reads=[B_PS[p]], writes=[B_vsn])

        sstep = [0]
        for i in range(NSS):
            S.op("dve", lambda: nc.vector.memset(Lac[:], 0.0), reads=[B_Lac], writes=[B_Lac])
            for q in range(2):
                S.op("dve", lambda q=q: nc.vector.memset(Lbs[q][:], 0.0), reads=[B_Lbs[q]], writes=[B_Lbs[q]])
            prev = 0
            nsteps = NPG + 1
            for st_i in range(nsteps):
                first = st_i == 0
                last = st_i == nsteps - 1
                z = sstep[0] % 2
                bi = z
                sstep[0] += 1
                if first:
                    nk = DSQ
                    kT_fn = lambda kc, i=i: ksT[:, kc, i * DSQ:(i + 1) * DSQ]
                    kreads = [B_ksT]
                    v_ap = vsn[0:DSQ, i, :]
                    vreads = [B_vsn]
                else:
                    nk = 128
                    j = NPG - st_i
                    col = i * NPG + j
                    pb = sstep[0] % 2
                    S.dma("pool", Kpg[pb][:, :], ck[:, :], reads=[B_idx], writes=[B_Kpg[pb]],
                          indirect=bass.IndirectOffsetOnAxis(ap=idx[:, col:col + 1], axis=0))
                    S.dma("pool", Vpg[pb][:, :], cv[:, :], reads=[B_idx], writes=[B_Vpg[pb]],
                          indirect=bass.IndirectOffsetOnAxis(ap=idx[:, col:col + 1], axis=0))
                    for c4 in range(2):
                        p = next_ps(4, 7)
                        for cc in range(4):
                            c = c4 * 4 + cc
                            S.op("pe", lambda c=c, cc=cc, p=p, pb=pb: nc.tensor.transpose(PS[p][:, cc * 128:(cc + 1) * 128],
                                                                                          Kpg[pb][:, c * 128:(c + 1) * 128], ident[:, :]),
                                 reads=[B_Kpg[pb], B_const], writes=[B_PS[p]], inc=(cc == 3))
                        S.op("dve", lambda c4=c4, p=p: nc.vector.tensor_copy(
                            out=KpT[:, c4 * 4:(c4 + 1) * 4, :].rearrange("p a b -> p (a b)"), in_=PS[p][:, :]),
                            reads=[B_PS[p]], writes=[B_KpT])
                    S.op("act", lambda pb=pb: nc.scalar.copy(out=Vpb[:, :], in_=Vpg[pb][:, :]), reads=[B_Vpg[pb]], writes=[B_Vpb])
                    kT_fn = lambda kc: KpT[:, kc, :]
                    kreads = [B_KpT]
                    v_ap = Vpb[:, :]
                    vreads = [B_Vpb]
                for kc in range(KC):
                    S.op("pe", lambda kc=kc, z=z, nk=nk, kT_fn=kT_fn, i=i: nc.tensor.matmul(
                        PS[z][0:nk, kc * 16:(kc + 1) * 16], lhsT=kT_fn(kc), rhs=Qbd[:, kc, i, :],
                        start=(kc == 0), stop=False, skip_group_check=True),
                        reads=kreads + [B_Qbd], writes=[B_PS[z]], inc=(kc == KC - 1))
                for brow_t in (bhi, blo):
                    S.op("pe", lambda z=z, nk=nk, brow_t=brow_t: nc.tensor.matmul(
                        PS[z][0:nk, 0:128], lhsT=one1[0:1, 0:nk], rhs=brow_t[0:1, :], start=False, stop=False, skip_group_check=True),
                        reads=[B_bias], writes=[B_PS[z]])
                S.op("act", lambda z=z, nk=nk, bi=bi: nc.scalar.activation(out=Es[bi][0:nk, :], in_=PS[z][0:nk, 0:128], func=AF.Exp, scale=0.125),
                     reads=[B_PS[z]], writes=[B_Es[bi]])
                S.op("act", lambda nk=nk, bi=bi: nc.scalar.activation(out=Ls[bi][0:nk, :], in_=Es[bi][0:nk, :], func=AF.Ln, bias=1.0, scale=1.0),
                     reads=[B_Es[bi]], writes=[B_Ls[bi]])
                if first:
                    S.op("dve", lambda bi=bi: nc.vector.tensor_tensor(out=Ls[bi][0:DSQ, :], in0=Ls[bi][0:DSQ, :], in1=mask8[:, :], op=ALU.mult),
                         reads=[B_Ls[bi], B_bias], writes=[B_Ls[bi]])
                S.op("pe", lambda z=z, nk=nk, bi=bi, first=first: nc.tensor.matmul(
                    PS[z][0:nk, 0:128], lhsT=tri8[0:nk, 0:nk], rhs=Ls[bi][0:nk, :], start=False, stop=first, skip_group_check=True),
                    reads=[B_Ls[bi], B_const], writes=[B_PS[z]])
                if not first:
                    S.op("pe", lambda z=z, nk=nk, prev=prev: nc.tensor.matmul(
                        PS[z][0:nk, 0:128], lhsT=one8[:, 0:nk], rhs=Lbs[prev][:, :], start=False, stop=True, skip_group_check=True),
                        reads=[B_Lbs[prev], B_const], writes=[B_PS[z]])
                if not last:
                    S.op("dve", lambda nk=nk, bi=bi: nc.vector.tensor_tensor(out=Lbs[bi][0:nk, :], in0=Lac[0:nk, :], in1=Ls[bi][0:nk, :], op=ALU.add),
                         reads=[B_Lac, B_Ls[bi]], writes=[B_Lbs[bi]])
                    S.op("dve", lambda nk=nk, bi=bi: nc.vector.tensor_tensor(out=Lac[0:nk, :], in0=Lac[0:nk, :], in1=Ls[bi][0:nk, :], op=ALU.add),
                         reads=[B_Lac, B_Ls[bi]], writes=[B_Lac])
                S.op("act", lambda z=z, nk=nk, bi=bi: nc.scalar.activation(out=As[bi][0:nk, :], in_=PS[z][0:nk, 0:128], func=AF.Exp, scale=0.125),
                     reads=[B_PS[z]], writes=[B_As[bi]])
                if first:
                    S.op("dve", lambda bi=bi: nc.vector.tensor_tensor(out=As[bi][0:DSQ, :], in0=As[bi][0:DSQ, :], in1=mask8[:, :], op=ALU.mult),
                         reads=[B_As[bi], B_bias], writes=[B_As[bi]])
                for kc in range(KC):
                    S.op("pe", lambda kc=kc, nk=nk, bi=bi, v_ap=v_ap, first=first, last=last: nc.tensor.matmul(
                        PS[2 + kc // 4][:, (kc % 4) * 128:(kc % 4 + 1) * 128], lhsT=v_ap[:, kc * 128:(kc + 1) * 128], rhs=As[bi][0:nk, :],
                        start=(first and kc % 4 == 0), stop=last, skip_group_check=True),
                        reads=vreads + [B_As[bi]], writes=[B_PS[2 + kc // 4]], inc=(kc % 4 == 3))
                prev = bi
            for kc in range(KC):
                for hp in range(2):
                    h = 2 * kc + hp
                    S.op("dve", lambda kc=kc, hp=hp, h=h, i=i: nc.vector.tensor_copy(
                        out=osT[hp * 64:(hp + 1) * 64, kc, i * DSQ:(i + 1) * DSQ],
                        in_=PS[2 + kc // 4][hp * 64:(hp + 1) * 64, (kc % 4) * 128 + h * 8:(kc % 4) * 128 + h * 8 + 8]),
                        reads=[B_PS[2 + kc // 4]], writes=[B_osT])
        ss.close()
        S.op("dve", lambda: nc.vector.tensor_copy(out=xT[:, :, 0:NS], in_=xsT[:, :, :]), reads=[B_xsT] + B_xT, writes=B_xT)
        layer1_tail(NS, osT, [B_osT] * KC, lambda b: ys[:, :], 1, NS)

    es.close()
    S.finish("sp")
    S.close()
    return nc


def make_in_maps(inputs, n_cores=8):
    f = lambda a: np.ascontiguousarray(np.asarray(a))
    x_prompt = f(inputs["x_prompt"])
    x_sample = f(inputs["x_sample"])
    cache_conv = f(inputs["cache_conv"])
    ckf = f(inputs["cache_k"])
    cvf = f(inputs["cache_v"])
    npool = ckf.shape[1]
    ck2 = ckf.reshape(npool * 128, D)
    cv2 = cvf.reshape(npool * 128, D)
    page_table = f(inputs["page_table"]).astype(np.int32)
    shared = {
        "ck": ck2, "cv": cv2,
        "mix_g": f(inputs["mix_norm_g"]), "ffn_g": f(inputs["ffn_norm_g"]), "fin_g": f(inputs["final_norm_g"]).reshape(1, D),
        "b_pw1": f(inputs["cv_b_pw1"]).reshape(2, D), "b_dw": f(inputs["cv_b_dw"]).reshape(1, D),
        "ln_g": f(inputs["cv_ln_g"]).reshape(1, D), "ln_b": f(inputs["cv_ln_b"]).reshape(1, D),
        "b_pw2": f(inputs["cv_b_pw2"]).reshape(1, D), "w_dw": f(inputs["cv_w_dw"]).reshape(CW, D),
        "sbb": f(inputs["sb_logit_bias"]).reshape(1, NH),
        "w_pw1": f(inputs["cv_w_pw1"])[0], "w_pw2": f(inputs["cv_w_pw2"])[0], "w_qkv": f(inputs["sb_w_qkv"])[0],
        "w_o": f(inputs["sb_w_o"])[0], "wg": f(inputs["ffn_w_gate"]), "wu": f(inputs["ffn_w_up"]), "wd": f(inputs["ffn_w_down"]),
    }
    in_maps = []
    for c in range(n_cores):
        m = dict(shared)
        m["xp"] = x_prompt[c // 2]
        m["xs"] = x_sample[NSS * c:NSS * (c + 1)].reshape(NS, D)
        m["cconv"] = cache_conv[0, NSS * c:NSS * (c + 1)].reshape(NSS * CS, D)
        m["pt"] = page_table[NSS * c:NSS * (c + 1)]
        in_maps.append(m)
    return in_maps


def kernel(**inputs):
    x_prompt = np.asarray(inputs["x_prompt"])
    BATCH, SEQ, _ = x_prompt.shape
    NPG = np.asarray(inputs["page_table"]).shape[1]
    NPOOL = np.asarray(inputs["cache_k"]).shape[1]
    nc = build(SEQ, NPG, NPOOL)
    in_maps = make_in_maps(inputs)
    res = run_bass_kernel_spmd(nc, in_maps, core_ids=list(range(8))).results
    ev = [res[2 * b] for b in range(BATCH)]
    y_prompt = np.stack([r["yp"] for r in ev]).astype(np.float32)
    y_sample = np.concatenate([r["ys"].reshape(NSS, DSQ, D) for r in res]).astype(np.float32)
    conv_p = np.stack([r["csp"] for r in ev])[None].astype(np.float32)
    conv_s = np.concatenate([r["css"].reshape(NSS, CS, D) for r in res])[None].astype(np.float32)
    k_p = np.stack([r["kp"].reshape(SEQ // 128, 128, NH, 64) for r in ev])[None].astype(np.float32)
    v_p = np.stack([r["vp"].reshape(SEQ // 128, 128, NH, 64) for r in ev])[None].astype(np.float32)
    k_s = np.concatenate([r["ks"].reshape(NSS, DSQ, NH, 64) for r in res])[None].astype(np.float32)
    v_s = np.concatenate([r["vs"].reshape(NSS, DSQ, NH, 64) for r in res])[None].astype(np.float32)
    return (y_prompt, y_sample, conv_p, conv_s, k_p, v_p, k_s, v_s)
```

```python
import numpy as np
import concourse.bass as bass
import concourse.mybir as mybir
from concourse.bass_utils import run_bass_kernel_spmd

F32 = mybir.dt.float32
BF16 = mybir.dt.bfloat16
I32 = mybir.dt.int32
AF = mybir.ActivationFunctionType
ALU = mybir.AluOpType

D = 1024
DFF = 2816
NH = 16
KC = 8
FC = 22
CW = 31
CS = 30
TW = 512
RMS_EPS = 1e-6
LN_EPS = 1e-5
NSS = 4
DSQ = 8
NS = NSS * DSQ


class Buf:
    __slots__ = ("name", "w", "rs", "dsem", "dcnt", "ssem", "scnt")

    def __init__(self, name):
        self.name = name
        self.w = None
        self.rs = []
        self.dsem = None
        self.dcnt = 0
        self.ssem = None
        self.scnt = 0


class Sched:
    def __init__(self, nc):
        self.nc = nc
        self.eng = {"pe": nc.tensor, "act": nc.scalar, "dve": nc.vector, "pool": nc.gpsimd, "sp": nc.sync}
        self.sem = {}
        self.cnt = {}
        self._ctx = []
        for k in self.eng:
            cm = nc.semaphore("prog_" + k)
            self.sem[k] = cm.__enter__()
            self._ctx.append(cm)
            self.cnt[k] = 0
        self.waited = {k: {} for k in self.eng}
        self.pending = {k: [] for k in self.eng}
        self.out_waits = []
        self.nsem = 0
        self.dma_latest = {}

    def new_sem(self, name):
        cm = self.nc.semaphore(name)
        s = cm.__enter__()
        self._ctx.append(cm)
        self.nsem += 1
        return s

    def close(self):
        for cm in reversed(self._ctx):
            cm.__exit__(None, None, None)

    def _wait(self, ek, sem, val):
        d = self.waited[ek]
        key = id(sem)
        if d.get(key, 0) >= val:
            return
        d[key] = val
        self.eng[ek].wait_ge(sem, val)

    def _deps(self, ek, reads, writes):
        for b in reads:
            if b.w is not None:
                self._wait(ek, b.w[0], b.w[1])
        for b in writes:
            if b.w is not None and b.w[2] != ek:
                self._wait(ek, b.w[0], b.w[1])
            for r in b.rs:
                if r[2] != ek:
                    self._wait(ek, r[0], r[1])

    def op(self, ek, fn, reads=(), writes=(), inc=True):
        self._deps(ek, reads, writes)
        ins = fn()
        if inc:
            self.cnt[ek] += 1
            ins.then_inc(self.sem[ek], 1)
            tag = (self.sem[ek], self.cnt[ek], ek)
            for b in self.pending[ek]:
                b.rs.append(tag)
            self.pending[ek] = []
            for b in reads:
                b.rs.append(tag)
            for b in writes:
                b.w = tag
                b.rs = []
        else:
            self.pending[ek].extend(reads)
            self.pending[ek].extend(writes)
        return ins

    def dma(self, qk, out_ap, in_ap, reads=(), writes=(), indirect=None):
        self._deps(qk, reads, writes)
        e = self.eng[qk]
        if indirect is not None:
            ins = e.indirect_dma_start(out=out_ap, out_offset=None, in_=in_ap, in_offset=indirect)
        else:
            ins = e.dma_start(out=out_ap, in_=in_ap)
        if writes:
            b = writes[0]
            if b.dsem is None:
                b.dsem = self.new_sem("d_" + b.name)
            b.dcnt += 1
            ins.then_inc(b.dsem, 16)
            tag = (b.dsem, 16 * b.dcnt, "dma")
            self.dma_latest[id(b.dsem)] = (b.dsem, 16 * b.dcnt)
            for wb in writes:
                wb.w = tag
                wb.rs = []
            for rb in reads:
                rb.rs.append(tag)
        else:
            b = reads[0]
            if b.ssem is None:
                b.ssem = self.new_sem("s_" + b.name)
            b.scnt += 1
            ins.then_inc(b.ssem, 16)
            tag = (b.ssem, 16 * b.scnt, "dma")
            self.dma_latest[id(b.ssem)] = (b.ssem, 16 * b.scnt)
            for rb in reads:
                rb.rs.append(tag)
            self.out_waits.append(tag)
        return ins

    def barrier(self):
        for ek in self.eng:
            for o in self.eng:
                if o != ek and self.cnt[o] > 0:
                    self._wait(ek, self.sem[o], self.cnt[o])
            for sem, val in self.dma_latest.values():
                self._wait(ek, sem, val)

    def finish(self, ek="sp"):
        last = {}
        for sem, val, _ in self.out_waits:
            k = id(sem)
            if k not in last or last[k][1] < val:
                last[k] = (sem, val)
        for sem, val in last.values():
            self._wait(ek, sem, val)


def build(SEQ, NPG, NPOOL, stage=99, sample=True):
    nc = bass.Bass("TRN2", target_bir_lowering=False)
    NT = SEQ // TW
    NBLK = SEQ // 128

    def din(name, shape, dt=F32):
        return nc.dram_tensor(name, list(shape), dt, kind="ExternalInput").ap()

    def dout(name, shape, dt=F32):
        return nc.dram_tensor(name, list(shape), dt, kind="ExternalOutput").ap()

    xp = din("xp", [SEQ, D])
    xs = din("xs", [NS, D])
    cconv = din("cconv", [NSS * CS, D])
    ck = din("ck", [NPOOL * 128, D])
    cv = din("cv", [NPOOL * 128, D])
    pt = din("pt", [NSS, NPG], I32)
    vec_in = {
        "mix_g": din("mix_g", [2, D]), "ffn_g": din("ffn_g", [2, D]), "fin_g": din("fin_g", [1, D]),
        "b_pw1": din("b_pw1", [2, D]), "b_dw": din("b_dw", [1, D]), "ln_g": din("ln_g", [1, D]),
        "ln_b": din("ln_b", [1, D]), "b_pw2": din("b_pw2", [1, D]), "w_dw": din("w_dw", [CW, D]),
    }
    sbb = din("sbb", [1, NH])
    w_pw1 = din("w_pw1", [D, 2 * D])
    w_pw2 = din("w_pw2", [D, D])
    w_qkv = din("w_qkv", [D, 3 * D])
    w_o = din("w_o", [D, D])
    wg = din("wg", [2, D, DFF])
    wu = din("wu", [2, D, DFF])
    wd = din("wd", [2, DFF, D])

    yp = dout("yp", [SEQ, D])
    ys = dout("ys", [NS, D])
    csp = dout("csp", [CS, D])
    css = dout("css", [NSS * CS, D])
    kp = dout("kp", [SEQ, D])
    vp = dout("vp", [SEQ, D])
    ks = dout("ks", [NS, D])
    vs = dout("vs", [NS, D])

    def scr(name, shape, dt):
        return nc.dram_tensor(name, list(shape), dt).ap()

    wsc = {}
    wsc["pw1"] = scr("wsc_pw1", [4, 128, 8, 512], BF16)
    wsc["pw2"] = scr("wsc_pw2", [2, 128, 8, 512], BF16)
    wsc["qkv"] = scr("wsc_qkv", [6, 128, 8, 512], BF16)
    wsc["wo"] = scr("wsc_wo", [2, 128, 8, 512], BF16)
    for l in range(2):
        wsc["wg%d" % l] = scr("wsc_wg%d" % l, [11, 128, 8, 256], BF16)
        wsc["wu%d" % l] = scr("wsc_wu%d" % l, [11, 128, 8, 256], BF16)
        wsc["wd%d" % l] = scr("wsc_wd%d" % l, [2, 3, 128, 8, 512], BF16)
    x1T = scr("x1T", [KC, 128, SEQ], F32)
    qTs = scr("qTs", [KC, 128, SEQ], BF16)
    kTs = scr("kTs", [KC, 128, SEQ], BF16)
    vtok = scr("vtok", [SEQ, D], BF16)
    oTs = scr("oTs", [KC, 128, SEQ], BF16)

    S = Sched(nc)
    B_wsc = {k: Buf("wsc_" + k) for k in wsc}
    B_x1T = [Buf("x1T%d" % c) for c in range(KC)]
    B_qTs = [Buf("qTs%d" % c) for c in range(KC)]
    B_kTs = [Buf("kTs%d" % c) for c in range(KC)]
    B_vtok = Buf("vtok")
    B_oTs = [Buf("oTs%d" % c) for c in range(KC)]

    from contextlib import ExitStack
    es = ExitStack()

    def sb(name, shape, dt, stack=None):
        return (stack or es).enter_context(nc.sbuf_tensor(name, list(shape), dt))

    def psum(name):
        return es.enter_context(nc.psum_tensor(name, [128, 512], F32))

    ident = sb("ident", [128, 128], F32)
    identb = sb("identb", [128, 128], BF16)
    trim = sb("trim", [128, 128], BF16)
    tri8 = sb("tri8", [128, 128], BF16)
    one8 = sb("one8", [128, 128], BF16)
    onesD = sb("onesD", [128, 128], BF16)
    scratchf = sb("scratchf", [128, 128], F32)
    vrows = sb("vrows", [128, 3, 128], F32)
    vT = sb("vT", [128, 3 * 128], F32)
    bias_h = sb("bias_h", [128, NH], F32)
    B_const = Buf("const")
    B_scr = Buf("scratchf")
    B_vrows = Buf("vrows")
    B_vT = Buf("vT")
    B_biash = Buf("bias_h")
    PS = [psum("ps%d" % i) for i in range(8)]
    B_PS = [Buf("ps%d" % i) for i in range(8)]

    g = nc.gpsimd
    S.op("pool", lambda: g.memset(ident[:], 1.0), writes=[B_const])
    S.op("pool", lambda: g.affine_select(out=ident[:], in_=ident[:], pattern=[[-1, 128]], compare_op=ALU.is_equal,
                                         fill=0.0, base=0, channel_multiplier=1), reads=[B_const], writes=[B_const])
    S.op("pool", lambda: g.tensor_copy(out=identb[:], in_=ident[:]), reads=[B_const], writes=[B_const])
    S.op("pool", lambda: g.memset(scratchf[:], 1.0), writes=[B_scr])
    S.op("pool", lambda: g.affine_select(out=scratchf[:], in_=scratchf[:], pattern=[[1, 128]], compare_op=ALU.is_gt,
                                         fill=0.0, base=0, channel_multiplier=-1), reads=[B_scr], writes=[B_scr])
    S.op("pool", lambda: g.tensor_copy(out=trim[:], in_=scratchf[:]), reads=[B_scr], writes=[B_const])
    S.op("pool", lambda: g.memset(scratchf[:], -8.0), reads=[B_scr], writes=[B_scr])
    S.op("pool", lambda: g.affine_select(out=scratchf[:], in_=scratchf[:], pattern=[[-1, 128]], compare_op=ALU.is_ge,
                                         fill=0.0, base=0, channel_multiplier=1), reads=[B_scr], writes=[B_scr])
    S.op("pool", lambda: g.tensor_copy(out=tri8[:], in_=scratchf[:]), reads=[B_scr], writes=[B_const])
    S.op("pool", lambda: g.memset(one8[:], -8.0), writes=[B_const])
    S.op("pool", lambda: g.memset(onesD[:], 1.0 / D), writes=[B_const])

    VBASE = {}
    r = 0
    for name, nrows in [("mix_g", 2), ("ffn_g", 2), ("fin_g", 1), ("b_pw1", 2), ("b_dw", 1), ("ln_g", 1),
                        ("ln_b", 1), ("b_pw2", 1), ("w_dw", CW)]:
        VBASE[name] = r
        src = vec_in[name].rearrange("r (c p) -> (r c) p", p=128)
        n = nrows * 8
        o = 0
        while o < n:
            grp, pp = divmod(r + o, 128)
            take = min(n - o, 128 - pp)
            S.dma("sp", vrows[pp:pp + take, grp, :], src[o:o + take, :], writes=[B_vrows])
            o += take
        r += n
    NVR = r
    S.op("pool", lambda: g.memset(vT[:], 0.0), writes=[B_vT])
    for grp in range(3):
        rows = min(128, NVR - grp * 128)
        S.op("pe", lambda grp=grp, rows=rows: nc.tensor.transpose(PS[7][:, 0:rows], vrows[0:rows, grp, :], ident[0:rows, 0:rows]),
             reads=[B_vrows, B_const], writes=[B_PS[7]])
        S.op("dve", lambda grp=grp, rows=rows: nc.vector.tensor_copy(out=vT[:, grp * 128:grp * 128 + rows], in_=PS[7][:, 0:rows]),
             reads=[B_PS[7]], writes=[B_vT])
    S.dma("sp", bias_h[:], sbb[0:1, :].partition_broadcast(128), writes=[B_biash])

    def vcol(name, row, c):
        k = VBASE[name] + row * 8 + c
        return vT[:, k:k + 1]

    with ExitStack() as st0:
        stg = [sb("stg%d" % i, [128, 3072], BF16, st0) for i in range(3)]
        B_stg = [Buf("stg%d" % i) for i in range(3)]
        si = [0]

        def precast(key, wap, din_rows, dcols, cb, down=False):
            for kc in range(din_rows // 128):
                i = si[0] % 3
                si[0] += 1
                S.dma("pool", stg[i][:, 0:dcols], wap[kc * 128:(kc + 1) * 128, :], writes=[B_stg[i]])
                if down:
                    kg, kcl = divmod(kc, 8)
                    dst = wsc[key][:, kg, :, kcl, :].rearrange("j p n -> p j n")
                else:
                    dst = wsc[key][:, :, kc, :].rearrange("j p n -> p j n")
                S.dma("sp", dst, stg[i][:, 0:dcols].rearrange("p (j n) -> p j n", n=cb), reads=[B_stg[i]], writes=[B_wsc[key]])

        precast("pw1", w_pw1, D, 2 * D, 512)
        precast("pw2", w_pw2, D, D, 512)
        for l in range(2):
            precast("wg%d" % l, wg[l], D, DFF, 256)
            precast("wu%d" % l, wu[l], D, DFF, 256)
            precast("wd%d" % l, wd[l], DFF, D, 512, down=True)
            if l == 0:
                precast("qkv", w_qkv, D, 3 * D, 512)
                precast("wo", w_o, D, D, 512)

    S.barrier()
    NSLOT = 6
    wslot = [sb("wslot%d" % i, [128, 8, 512], BF16) for i in range(NSLOT)]
    B_wslot = [Buf("wslot%d" % i) for i in range(NSLOT)]

    def blk_src(key):
        m = key[0]
        if m.startswith("wd"):
            _, jj, kg = key
            nk = 8 if kg < 2 else FC - 16
            return wsc[m][jj, kg, :, 0:nk, :], nk, 512
        if m.startswith("wg") or m.startswith("wu"):
            return wsc[m][key[1]], 8, 256
        return wsc[m][key[1]], 8, 512

    def ffn_blocks(l):
        out = []
        for jb in range(11):
            out.append(("wg%d" % l, jb))
            out.append(("wu%d" % l, jb))
        for jj in range(2):
            for kg in range(3):
                out.append(("wd%d" % l, jj, kg))
        return out

    L0_BLOCKS = [("pw1", 0), ("pw1", 2), ("pw1", 1), ("pw1", 3), ("pw2", 0), ("pw2", 1)] + ffn_blocks(0) + \
                [("qkv", j) for j in range(6)]
    L1_BLOCKS = [("wo", 0), ("wo", 1)] + ffn_blocks(1)
    plan = []
    for t in range(NT + 1):
        plan += L0_BLOCKS
    for t in range(NT + 1):
        plan += L1_BLOCKS
    wpos = [0, 0]

    def w_issue_upto(n):
        while wpos[1] < min(n, len(plan)):
            i = wpos[1]
            key = plan[i]
            src, nk, cb = blk_src(key)
            s = i % NSLOT
            S.dma("sp", wslot[s][:, 0:nk, 0:cb], src, reads=[B_wsc[key[0]]], writes=[B_wslot[s]])
            wpos[1] += 1

    def w_next(key):
        i = wpos[0]
        assert plan[i] == key, (i, plan[i], key)
        w_issue_upto(i + NSLOT - 2)
        wpos[0] += 1
        s = i % NSLOT
        return wslot[s], B_wslot[s]

    xtok = sb("xtok", [128, 4, D], F32)
    acc = sb("acc", [128, KC, TW], F32)
    xT = sb("xT", [128, KC, TW], F32)
    hT = [sb("hT%d" % i, [128, KC, TW], BF16) for i in range(2)]
    hid = sb("hid", [128, FC, TW], BF16)
    sq = sb("sq", [128, 2, TW], BF16)
    rs = sb("rs", [128, TW], F32)
    mean = sb("mean", [128, TW], F32)
    sg = [sb("sg%d" % i, [128, TW], F32) for i in range(2)]
    B_xtok = [Buf("xtok%d" % b) for b in range(4)]
    B_acc = [Buf("acc%d" % c) for c in range(KC)]
    B_xT = [Buf("xT%d" % c) for c in range(KC)]
    B_hT = [[Buf("hT%d_%d" % (i, c)) for c in range(KC)] for i in range(2)]
    B_uT = [Buf("uT%d" % c) for c in range(KC)]
    B_hid = [Buf("hid%d" % c) for c in range(FC)]
    B_sq = [Buf("sq%d" % i) for i in range(2)]
    B_rs = Buf("rs")
    B_mean = Buf("mean")
    B_sg = [Buf("sg%d" % i) for i in range(2)]
    B_vb16 = Buf("vb16")
    xsT = sb("xsT", [128, KC, NS], F32)
    B_xsT = Buf("xsT")
    B_cpre = Buf("cpre")
    qsT = sb("qsT", [128, KC, NS], BF16)
    ksT = sb("ksT", [128, KC, NS], BF16)
    vs32 = sb("vs32", [128, KC, NS], F32)
    B_qsT = Buf("qsT")
    B_ksT = Buf("ksT")
    B_vs32 = [Buf("vs32_%d" % c) for c in range(KC)]

    osT = sb("osT", [128, KC, NS], BF16)
    B_cst = Buf("cst")
    e0 = ExitStack()
    cst = sb("cst", [32, D], F32, e0)
    cpre = sb("cpre", [128, KC, NSS, CS], F32, e0)
    ks32 = sb("ks32", [128, KC, NS], F32, e0)
    B_ks32 = [Buf("ks32_%d" % c) for c in range(KC)]
    uT = sb("uT", [128, KC, CS + TW], F32, e0)
    vb16 = sb("vb16", [128, D], BF16, e0)

    def done():
        e0.close()
        es.close()
        S.finish("sp")
        S.close()
        return nc

    rr = [0]

    def next_ps(lo=0, hi=7):
        i = lo + rr[0] % (hi - lo)
        rr[0] += 1
        return i

    ev = [0]

    def evac_eng():
        ev[0] += 1
        return "act" if ev[0] % 2 else "dve"

    def load_tokmajor_T(src_rows_fn, W, dstT, B_dst, nblk, rows_per_blk):
        for b in range(nblk):
            S.dma("sp", xtok[0:rows_per_blk, b, :], src_rows_fn(b), writes=[B_xtok[b]])
        for c in range(KC):
            p = next_ps()
            for b in range(nblk):
                S.op("pe", lambda b=b, c=c, p=p: nc.tensor.transpose(
                    PS[p][:, b * 128:b * 128 + rows_per_blk], xtok[0:rows_per_blk, b, c * 128:(c + 1) * 128],
                    ident[0:rows_per_blk, 0:rows_per_blk]),
                    reads=[B_xtok[b], B_const], writes=[B_PS[p]], inc=(b == nblk - 1))
            e = evac_eng()
            if e == "act":
                S.op("act", lambda c=c, p=p: nc.scalar.copy(out=dstT[:, c, 0:W], in_=PS[p][:, 0:W]),
                     reads=[B_PS[p]], writes=[B_dst[c]])
            else:
                S.op("dve", lambda c=c, p=p: nc.vector.tensor_copy(out=dstT[:, c, 0:W], in_=PS[p][:, 0:W]),
                     reads=[B_PS[p]], writes=[B_dst[c]])

    def store_T_tokmajor(srcT, B_src, W, dst_rows_fn, nblk, rows_per_blk, extra_reads=()):
        for b in range(nblk):
            for c4 in range(2):
                p = next_ps()
                for cc in range(4):
                    c = c4 * 4 + cc
                    S.op("pe", lambda b=b, c=c, cc=cc, p=p: nc.tensor.transpose(
                        PS[p][0:rows_per_blk, cc * 128:(cc + 1) * 128], srcT[:, c, b * 128:b * 128 + rows_per_blk], ident[:, :]),
                        reads=[B_src[c], B_const], writes=[B_PS[p]], inc=(cc == 3))
                e = evac_eng()
                if e == "act":
                    S.op("act", lambda b=b, c4=c4, p=p: nc.scalar.copy(out=xtok[0:rows_per_blk, b, c4 * 512:(c4 + 1) * 512],
                                                                       in_=PS[p][0:rows_per_blk, :]),
                         reads=[B_PS[p]], writes=[B_xtok[b]])
                else:
                    S.op("dve", lambda b=b, c4=c4, p=p: nc.vector.tensor_copy(out=xtok[0:rows_per_blk, b, c4 * 512:(c4 + 1) * 512],
                                                                              in_=PS[p][0:rows_per_blk, :]),
                         reads=[B_PS[p]], writes=[B_xtok[b]])
            S.dma("pool", dst_rows_fn(b), xtok[0:rows_per_blk, b, :], reads=[B_xtok[b]])

    def rmsnorm(srcT, B_src, W, gname, grow, dstT, B_dst):
        p = next_ps()
        for c in range(KC):
            i = c % 2
            S.op("act", lambda c=c, i=i: nc.scalar.activation(out=sq[:, i, 0:W], in_=srcT[:, c, 0:W], func=AF.Square),
                 reads=[B_src[c]], writes=[B_sq[i]])
            S.op("pe", lambda c=c, i=i, p=p: nc.tensor.matmul(PS[p][:, 0:W], lhsT=onesD[:], rhs=sq[:, i, 0:W],
                                                              start=(c == 0), stop=(c == KC - 1)),
                 reads=[B_sq[i], B_const], writes=[B_PS[p]])
        S.op("act", lambda p=p: nc.scalar.activation(out=rs[:, 0:W], in_=PS[p][:, 0:W], func=AF.Sqrt, bias=RMS_EPS, scale=1.0),
             reads=[B_PS[p]], writes=[B_rs])
        S.op("dve", lambda: nc.vector.reciprocal(out=rs[:, 0:W], in_=rs[:, 0:W]), reads=[B_rs], writes=[B_rs])
        for c in range(KC):
            S.op("dve", lambda c=c: nc.vector.scalar_tensor_tensor(out=dstT[:, c, 0:W], in0=srcT[:, c, 0:W],
                                                                   scalar=vcol(gname, grow, c), in1=rs[:, 0:W],
                                                                   op0=ALU.mult, op1=ALU.mult),
                 reads=[B_src[c], B_rs, B_vT], writes=[B_dst[c]])

    def mm_group(p, W, wt, B_w, col0, inT, B_in, nk, kc0=0, start=True, stop=True, extra_w=()):
        for k in range(nk):
            last = (k == nk - 1)
            S.op("pe", lambda k=k: nc.tensor.matmul(PS[p][:, 0:W], lhsT=wt[:, k, col0:col0 + 128], rhs=inT[:, kc0 + k, 0:W],
                                                    start=(start and k == 0), stop=(stop and last)),
                 reads=[B_w, B_in[kc0 + k]], writes=[B_PS[p]], inc=last)

    def ffn(l, W, inT, B_in, resT, B_res):
        for jb in range(11):
            wgt, Bg = w_next(("wg%d" % l, jb))
            wut, Bu = w_next(("wu%d" % l, jb))
            for half in range(2):
                fc = jb * 2 + half
                pg = next_ps()
                mm_group(pg, W, wgt, Bg, half * 128, inT, B_in, KC)
                pu = next_ps()
                mm_group(pu, W, wut, Bu, half * 128, inT, B_in, KC)
                i = fc % 2
                S.op("act", lambda pg=pg, i=i: nc.scalar.activation(out=sg[i][:, 0:W], in_=PS[pg][:, 0:W], func=AF.Silu),
                     reads=[B_PS[pg]], writes=[B_sg[i]])
                S.op("dve", lambda pu=pu, i=i, fc=fc: nc.vector.tensor_tensor(out=hid[:, fc, 0:W], in0=PS[pu][:, 0:W],
                                                                              in1=sg[i][:, 0:W], op=ALU.mult),
                     reads=[B_PS[pu], B_sg[i]], writes=[B_hid[fc]])
        for jj in range(2):
            wts = [w_next(("wd%d" % l, jj, kg)) for kg in range(3)]
            for oc4 in range(4):
                oc = jj * 4 + oc4
                p = next_ps()
                for kg in range(3):
                    nk = 8 if kg < 2 else FC - 16
                    mm_group(p, W, wts[kg][0], wts[kg][1], oc4 * 128, hid, B_hid, nk, kc0=kg * 8,
                             start=(kg == 0), stop=(kg == 2))
                S.op("dve", lambda oc=oc, p=p: nc.vector.tensor_tensor(out=resT[:, oc, 0:W], in0=resT[:, oc, 0:W],
                                                                       in1=PS[p][:, 0:W], op=ALU.add),
                     reads=[B_PS[p], B_res[oc]], writes=[B_res[oc]])

    def conv_mixer(W, curT, B_cur, hin, B_hin, hout, B_hout, segs):
        for half in range(2):
            wa, Ba = w_next(("pw1", half))
            wgt, Bg = w_next(("pw1", 2 + half))
            for oc4 in range(4):
                oc = half * 4 + oc4
                pa = next_ps()
                mm_group(pa, W, wa, Ba, oc4 * 128, hin, B_hin, KC)
                pg = next_ps()
                mm_group(pg, W, wgt, Bg, oc4 * 128, hin, B_hin, KC)
                i = oc % 2
                S.op("act", lambda pg=pg, i=i, oc=oc: nc.scalar.activation(out=sg[i][:, 0:W], in_=PS[pg][:, 0:W], func=AF.Sigmoid,
                                                                           bias=vcol("b_pw1", 1, oc), scale=1.0),
                     reads=[B_PS[pg], B_vT], writes=[B_sg[i]])
                S.op("dve", lambda pa=pa, i=i, oc=oc: nc.vector.scalar_tensor_tensor(
                    out=uT[:, oc, CS:CS + W], in0=PS[pa][:, 0:W], scalar=vcol("b_pw1", 0, oc), in1=sg[i][:, 0:W],
                    op0=ALU.add, op1=ALU.mult), reads=[B_PS[pa], B_sg[i], B_vT], writes=[B_uT[oc]])


    def conv_all(items):
        for k in range(CW):
            for ek, c, src_fn, dst_ap, reads, writes in items:
                e = nc.vector if ek == "dve" else nc.gpsimd
                if k == 0:
                    S.op(ek, lambda e=e, c=c, src_fn=src_fn, dst_ap=dst_ap: e.tensor_scalar(
                        out=dst_ap, in0=src_fn(0), scalar1=vcol("w_dw", 0, c), scalar2=vcol("b_dw", 0, c), op0=ALU.mult, op1=ALU.add),
                        reads=reads + [B_vT], writes=writes)
                else:
                    S.op(ek, lambda e=e, c=c, k=k, src_fn=src_fn, dst_ap=dst_ap: e.scalar_tensor_tensor(
                        out=dst_ap, in0=src_fn(k), scalar=vcol("w_dw", k, c), in1=dst_ap, op0=ALU.mult, op1=ALU.add),
                        reads=reads + [B_vT], writes=writes)

    def ln_silu(W, srcT, B_src, dstT, B_dst):
        pm = next_ps()
        pq = next_ps()
        for c in range(KC):
            i = c % 2
            S.op("act", lambda c=c, i=i: nc.scalar.activation(out=sq[:, i, 0:W], in_=srcT[:, c, 0:W], func=AF.Square),
                 reads=[B_src[c]], writes=[B_sq[i]])
            S.op("pe", lambda c=c, i=i: nc.tensor.matmul(PS[pq][:, 0:W], lhsT=onesD[:], rhs=sq[:, i, 0:W],
                                                         start=(c == 0), stop=(c == KC - 1)),
                 reads=[B_sq[i], B_const], writes=[B_PS[pq]])
        for c in range(KC):
            i = c % 2
            S.op("dve", lambda c=c, i=i: nc.vector.tensor_copy(out=hT[1][:, i, 0:W], in_=srcT[:, c, 0:W]),
                 reads=[B_src[c]], writes=[B_hT[1][i]])
            S.op("pe", lambda c=c, i=i: nc.tensor.matmul(PS[pm][:, 0:W], lhsT=onesD[:], rhs=hT[1][:, i, 0:W],
                                                         start=(c == 0), stop=(c == KC - 1)),
                 reads=[B_hT[1][i], B_const], writes=[B_PS[pm]])
        S.op("dve", lambda: nc.vector.tensor_copy(out=mean[:, 0:W], in_=PS[pm][:, 0:W]), reads=[B_PS[pm]], writes=[B_mean])
        S.op("dve", lambda: nc.vector.tensor_tensor(out=rs[:, 0:W], in0=mean[:, 0:W], in1=mean[:, 0:W], op=ALU.mult),
             reads=[B_mean], writes=[B_rs])
        S.op("dve", lambda: nc.vector.tensor_tensor(out=rs[:, 0:W], in0=PS[pq][:, 0:W], in1=rs[:, 0:W], op=ALU.subtract),
             reads=[B_PS[pq], B_rs], writes=[B_rs])
        S.op("dve", lambda: nc.vector.tensor_scalar(out=rs[:, 0:W], in0=rs[:, 0:W], scalar1=0.0, scalar2=None, op0=ALU.max),
             reads=[B_rs], writes=[B_rs])
        S.op("act", lambda: nc.scalar.activation(out=rs[:, 0:W], in_=rs[:, 0:W], func=AF.Sqrt, bias=LN_EPS, scale=1.0),
             reads=[B_rs], writes=[B_rs])
        S.op("dve", lambda: nc.vector.reciprocal(out=rs[:, 0:W], in_=rs[:, 0:W]), reads=[B_rs], writes=[B_rs])
        for c in range(KC):
            S.op("dve", lambda c=c: nc.vector.tensor_tensor(out=srcT[:, c, 0:W], in0=srcT[:, c, 0:W], in1=mean[:, 0:W], op=ALU.subtract),
                 reads=[B_src[c], B_mean], writes=[B_src[c]])
            S.op("dve", lambda c=c: nc.vector.tensor_tensor(out=srcT[:, c, 0:W], in0=srcT[:, c, 0:W], in1=rs[:, 0:W], op=ALU.mult),
                 reads=[B_src[c], B_rs], writes=[B_src[c]])
            S.op("act", lambda c=c: nc.scalar.activation(out=dstT[:, c, 0:W], in_=srcT[:, c, 0:W], func=AF.Silu,
                                                         bias=vcol("ln_b", 0, c), scale=vcol("ln_g", 0, c)),
                 reads=[B_src[c], B_vT], writes=[B_dst[c]])

    def pw2_residual(W, inT, B_in, resT, B_res):
        for jj in range(2):
            wt, Bw = w_next(("pw2", jj))
            for oc4 in range(4):
                oc = jj * 4 + oc4
                p = next_ps()
                mm_group(p, W, wt, Bw, oc4 * 128, inT, B_in, KC)
                S.op("dve", lambda oc=oc, p=p: nc.vector.scalar_tensor_tensor(
                    out=resT[:, oc, 0:W], in0=PS[p][:, 0:W], scalar=vcol("b_pw2", 0, oc), in1=resT[:, oc, 0:W],
                    op0=ALU.add, op1=ALU.add), reads=[B_PS[p], B_res[oc], B_vT], writes=[B_res[oc]])

    def qkv_tile(W, hin, B_hin, tok0, nblk, rows, qT_dst, kT_dst, B_qdst, B_kdst, k_out, v_out, v_bf_dst, B_vbf):
        for which, dst_fn, Bd in (("q", qT_dst, B_qdst), ("k", kT_dst, B_kdst)):
            for jj in range(2):
                wt, Bw = w_next(("qkv", (0 if which == "q" else 2) + jj))
                if which == "k":
                    kw[jj] = (wt, Bw)
                for oc4 in range(4):
                    oc = jj * 4 + oc4
                    p = next_ps()
                    mm_group(p, W, wt, Bw, oc4 * 128, hin, B_hin, KC)
                    i = oc % 2
                    dst_fn(oc, p, i)
        accv = acc[:].rearrange("p c t -> p (c t)")
        for which in ("k", "v"):
            wl = kw if which == "k" else [w_next(("qkv", 4 + jj)) for jj in range(2)]
            for b in range(nblk):
                for jj in range(2):
                    wt, Bw = wl[jj]
                    p = next_ps()
                    for k in range(KC):
                        S.op("pe", lambda k=k, wt=wt, p=p, b=b: nc.tensor.matmul(
                            PS[p][0:rows, :], lhsT=hin[:, k, b * 128:b * 128 + rows], rhs=wt[:, k, :],
                            start=(k == 0), stop=(k == KC - 1)), reads=[Bw, B_hin[k]], writes=[B_PS[p]], inc=(k == KC - 1))
                    if which == "k":
                        S.op("act", lambda p=p, b=b, jj=jj: nc.scalar.copy(out=xtok[0:rows, b, jj * 512:(jj + 1) * 512], in_=PS[p][0:rows, :]),
                             reads=[B_PS[p]], writes=[B_xtok[b]])
                    else:
                        S.op("act", lambda p=p, b=b, jj=jj: nc.scalar.copy(
                            out=accv[0:rows, b * D + jj * 512:b * D + (jj + 1) * 512], in_=PS[p][0:rows, :]),
                            reads=[B_PS[p]], writes=[B_acc[2 * b], B_acc[2 * b + 1]])
        for b in range(nblk):
            S.dma("pool", k_out(b), xtok[0:rows, b, :], reads=[B_xtok[b]])
            S.dma("pool", v_out(b), accv[0:rows, b * D:(b + 1) * D], reads=[B_acc[2 * b], B_acc[2 * b + 1]])
            v_bf_dst(b, accv)

    kw = [None, None]

    def tokmajor_out(srcT_fn, B_src, ncols, dst_ap):
        for c4 in range(2):
            p = next_ps()
            for cc in range(4):
                c = c4 * 4 + cc
                S.op("pe", lambda c=c, cc=cc, p=p: nc.tensor.transpose(PS[p][0:ncols, cc * 128:(cc + 1) * 128],
                                                                       srcT_fn(c), ident[:, :]),
                     reads=[B_src[c], B_const], writes=[B_PS[p]], inc=(cc == 3))
            S.op("dve", lambda c4=c4, p=p: nc.vector.tensor_copy(out=cst[0:ncols, c4 * 512:(c4 + 1) * 512], in_=PS[p][0:ncols, :]),
                 reads=[B_PS[p]], writes=[B_cst])
        S.dma("pool", dst_ap, cst[0:ncols, :], reads=[B_cst])

    def layer0_tail(W, c0, is_sample):
        ln_silu(W, acc, B_acc, hT[0], B_hT[0])
        pw2_residual(W, hT[0], B_hT[0], xT, B_xT)
        rmsnorm(xT, B_xT, W, "ffn_g", 0, hT[1], B_hT[1])
        ffn(0, W, hT[1], B_hT[1], xT, B_xT)

    for t in range(NT):
        W = TW
        c0 = t * TW
        load_tokmajor_T(lambda b, c0=c0: xp[c0 + b * 128:c0 + (b + 1) * 128, :], W, xT, B_xT, 4, 128)
        if stage == 0:
            return done()
        rmsnorm(xT, B_xT, W, "mix_g", 0, hT[0], B_hT[0])
        if stage == 0.1:
            return done()
        for c in range(KC):
            if t == 0:
                S.op("dve", lambda c=c: nc.vector.memset(uT[:, c, 0:CS], 0.0), writes=[B_uT[c]])
            else:
                S.op("dve", lambda c=c: nc.vector.tensor_copy(out=uT[:, c, 0:CS], in_=uT[:, c, TW:TW + CS]),
                     reads=[B_uT[c]], writes=[B_uT[c]])
        conv_mixer(W, xT, B_xT, hT[0], B_hT[0], None, None, None)
        if stage == 0.2:
            return done()
        if t == NT - 1:
            tokmajor_out(lambda c: uT[:, c, TW:TW + CS], B_uT, CS, csp[:, :])
        if stage == 0.25:
            return done()
        conv_all([("dve", c, (lambda k, c=c: uT[:, c, k:k + W]), acc[:, c, 0:W], [B_uT[c]], [B_acc[c]])
                  for c in range(KC)])
        if stage == 0.3:
            return done()
        layer0_tail(W, c0, False)
        if stage == 0.4:
            return done()
        for c in range(KC):
            S.dma("pool", x1T[c, :, c0:c0 + W], xT[:, c, 0:W], reads=[B_xT[c]], writes=[B_x1T[c]])
        rmsnorm(xT, B_xT, W, "mix_g", 1, hT[0], B_hT[0])

        def q_dst(oc, p, i, c0=c0, W=W):
            S.op("act", lambda: nc.scalar.copy(out=hT[1][:, oc, 0:W], in_=PS[p][:, 0:W]), reads=[B_PS[p]], writes=[B_hT[1][oc]])
            S.dma("pool", qTs[oc, :, c0:c0 + W], hT[1][:, oc, 0:W], reads=[B_hT[1][oc]], writes=[B_qTs[oc]])

        def k_dst(oc, p, i, c0=c0, W=W):
            S.op("dve", lambda: nc.vector.tensor_copy(out=hid[:, oc, 0:W], in_=PS[p][:, 0:W]), reads=[B_PS[p]], writes=[B_hid[oc]])
            S.dma("pool", kTs[oc, :, c0:c0 + W], hid[:, oc, 0:W], reads=[B_hid[oc]], writes=[B_kTs[oc]])

        def v_bf(b, accv, c0=c0):
            S.op("dve", lambda: nc.vector.tensor_copy(out=vb16[:, :], in_=accv[:, b * D:(b + 1) * D]),
                 reads=[B_acc[2 * b], B_acc[2 * b + 1]], writes=[B_vb16])
            S.dma("pool", vtok[c0 + b * 128:c0 + (b + 1) * 128, :], vb16[:, :], reads=[B_vb16], writes=[B_vtok])

        qkv_tile(W, hT[0], B_hT[0], c0, 4, 128, q_dst, k_dst, B_hT[1], B_hid,
                 lambda b, c0=c0: kp[c0 + b * 128:c0 + (b + 1) * 128, :],
                 lambda b, c0=c0: vp[c0 + b * 128:c0 + (b + 1) * 128, :], v_bf, B_vb16)

    if stage == 0.5:
        return done()
    if sample:
        W = NS
        load_tokmajor_T(lambda b: xs[:, :], W, xT, B_xT, 1, NS)
        rmsnorm(xT, B_xT, W, "mix_g", 0, hT[0], B_hT[0])
        conv_mixer(W, xT, B_xT, hT[0], B_hT[0], None, None, None)
        for i in range(NSS):
            S.dma("sp", xtok[0:CS, i, :], cconv[i * CS:(i + 1) * CS, :], writes=[B_xtok[i]])
        for c in range(KC):
            p = next_ps()
            for i in range(NSS):
                S.op("pe", lambda c=c, i=i, p=p: nc.tensor.transpose(PS[p][:, i * 32:i * 32 + CS], xtok[0:CS, i, c * 128:(c + 1) * 128],
                                                                     ident[0:CS, 0:CS]),
                     reads=[B_xtok[i], B_const], writes=[B_PS[p]], inc=(i == NSS - 1))
            S.op("dve", lambda c=c, p=p: nc.vector.tensor_copy(
                out=cpre[:, c, :, :], in_=PS[p][:, 0:NSS * 32].rearrange("p (i t) -> p i t", t=32)[:, :, 0:CS]),
                reads=[B_PS[p]], writes=[B_cpre])
        if stage == 0.6:
            return done()
        XO = 64
        for c in range(KC):
            for i in range(NSS):
                o = XO + i * 40
                S.op("dve", lambda c=c, i=i, o=o: nc.vector.tensor_copy(out=uT[:, c, o:o + CS], in_=cpre[:, c, i, :]),
                     reads=[B_cpre, B_uT[c]], writes=[B_uT[c]])
                S.op("dve", lambda c=c, i=i, o=o: nc.vector.tensor_copy(out=uT[:, c, o + CS:o + CS + DSQ],
                                                                        in_=uT[:, c, CS + i * DSQ:CS + (i + 1) * DSQ]),
                     reads=[B_uT[c]], writes=[B_uT[c]])
        for i in range(NSS):
            o = XO + i * 40
            tokmajor_out(lambda c, o=o: uT[:, c, o + DSQ:o + DSQ + CS], B_uT, CS, css[i * CS:(i + 1) * CS, :])
        if stage == 0.7:
            return done()
        conv_all([("dve", c, (lambda k, c=c, o=XO + i * 40: uT[:, c, o + k:o + k + DSQ]), acc[:, c, i * DSQ:(i + 1) * DSQ],
                   [B_uT[c]], [B_acc[c]]) for c in range(KC) for i in range(NSS)])
        if stage == 0.75:
            return done()
        layer0_tail(W, 0, True)
        if stage == 0.8:
            return done()
        SAMPLE_QKV = True
        if SAMPLE_QKV:
            S.op("dve", lambda: nc.vector.tensor_copy(out=xsT[:, :, :], in_=xT[:, :, 0:NS]), reads=B_xT, writes=[B_xsT])
            rmsnorm(xT, B_xT, W, "mix_g", 1, hT[0], B_hT[0])

            for which, base in (("q", 0), ("k", 2), ("v", 4)):
                for jj in range(2):
                    wt, Bw = w_next(("qkv", base + jj))
                    for oc4 in range(4):
                        oc = jj * 4 + oc4
                        p = next_ps()
                        mm_group(p, NS, wt, Bw, oc4 * 128, hT[0], B_hT[0], KC)
                        if which == "q":
                            S.op("act", lambda oc=oc, p=p: nc.scalar.copy(out=qsT[:, oc, :], in_=PS[p][:, 0:NS]),
                                 reads=[B_PS[p]], writes=[B_qsT])
                        elif which == "k":
                            S.op("act", lambda oc=oc, p=p: nc.scalar.copy(out=ks32[:, oc, :], in_=PS[p][:, 0:NS]),
                                 reads=[B_PS[p]], writes=[B_ks32[oc]])
                            S.op("dve", lambda oc=oc, p=p: nc.vector.tensor_copy(out=ksT[:, oc, :], in_=PS[p][:, 0:NS]),
                                 reads=[B_PS[p]], writes=[B_ksT])
                        else:
                            S.op("act", lambda oc=oc, p=p: nc.scalar.copy(out=vs32[:, oc, :], in_=PS[p][:, 0:NS]),
                                 reads=[B_PS[p]], writes=[B_vs32[oc]])
            tokmajor_out(lambda c: ks32[:, c, :], B_ks32, NS, ks[:, :])
            tokmajor_out(lambda c: vs32[:, c, :], B_vs32, NS, vs[:, :])
        else:
            wpos[0] += 6
            w_issue_upto(wpos[0])
    if not sample:
        wpos[0] += len(L0_BLOCKS)
        w_issue_upto(wpos[0])
    e0.close()
    S.barrier()

    sa = ExitStack()
    KTc = sb("KTc", [128, SEQ], BF16, sa)
    QTc = sb("QTc", [128, SEQ], BF16, sa)
    Vc = sb("Vc", [128, NBLK, 128], BF16, sa)
    OTc = sb("OTc", [128, SEQ], BF16, sa)
    Et = [sb("Et%d" % i, [128, TW], F32, sa) for i in range(2)]
    Lt = [sb("Lt%d" % i, [128, TW], BF16, sa) for i in range(3)]
    At = [sb("At%d" % i, [128, TW], BF16, sa) for i in range(3)]
    Lb = [sb("Lb%d" % i, [128, TW], BF16, sa) for i in range(3)]
    Lacc = sb("Lacc", [128, TW], F32, sa)
    B_KTc, B_QTc, B_Vc, B_OTc, B_Lacc = Buf("KTc"), Buf("QTc"), Buf("Vc"), Buf("OTc"), Buf("Lacc")
    B_Et = [Buf("Et%d" % i) for i in range(2)]
    B_Lt = [Buf("Lt%d" % i) for i in range(3)]
    B_At = [Buf("At%d" % i) for i in range(3)]
    B_Lb = [Buf("Lb%d" % i) for i in range(3)]
    ZB = [0, 1, 4]
    unitc = [0]

    def att_s1(k, st):
        z, e, l = ZB[k % 3], k % 2, k % 3
        a0 = st["a0"]
        cols = slice(a0, TW)
        pr, j, qi, h = st["pr"], st["j"], st["qi"], st["h"]
        if st["first"]:
            S.op("dve", lambda: nc.vector.memset(Lacc[:, :], 0.0), reads=[B_Lacc], writes=[B_Lacc])
        S.op("pe", lambda: nc.tensor.matmul(PS[z][:, cols], lhsT=KTc[pr, j * 128:(j + 1) * 128], rhs=QTc[pr, qi * TW + a0:(qi + 1) * TW],
                                            start=True, stop=False), reads=[B_KTc, B_QTc], writes=[B_PS[z]])
        S.op("act", lambda: nc.scalar.activation(out=Et[e][:, cols], in_=PS[z][:, cols], func=AF.Exp, bias=bias_h[:, h:h + 1], scale=0.125),
             reads=[B_PS[z], B_biash], writes=[B_Et[e]])
        S.op("act", lambda: nc.scalar.activation(out=Lt[l][:, cols], in_=Et[e][:, cols], func=AF.Ln, bias=1.0, scale=1.0),
             reads=[B_Et[e]], writes=[B_Lt[l]])
        if st["diag"]:
            S.op("dve", lambda: nc.vector.tensor_tensor(out=Lt[l][:, a0:a0 + 128], in0=Lt[l][:, a0:a0 + 128], in1=trim[:, :], op=ALU.mult),
                 reads=[B_Lt[l], B_const], writes=[B_Lt[l]])
        if not st["last"]:
            S.op("dve", lambda: nc.vector.tensor_tensor(out=Lb[l][:, cols], in0=Lacc[:, cols], in1=Lt[l][:, cols], op=ALU.add),
                 reads=[B_Lacc, B_Lt[l]], writes=[B_Lb[l]])
            S.op("dve", lambda: nc.vector.tensor_tensor(out=Lacc[:, cols], in0=Lacc[:, cols], in1=Lt[l][:, cols], op=ALU.add),
                 reads=[B_Lacc, B_Lt[l]], writes=[B_Lacc])

    def att_s2(k, st):
        z, l, lp = ZB[k % 3], k % 3, (k - 1) % 3
        a0 = st["a0"]
        cols = slice(a0, TW)
        h = st["h"]
        if not st["has_carry"]:
            S.op("pe", lambda: nc.tensor.matmul(PS[z][:, cols], lhsT=tri8[:, :], rhs=Lt[l][:, cols], start=False, stop=True),
                 reads=[B_Lt[l], B_const], writes=[B_PS[z]])
        else:
            clo = st["carry_lo"]
            if clo > a0:
                S.op("pe", lambda: nc.tensor.matmul(PS[z][:, a0:clo], lhsT=tri8[:, :], rhs=Lt[l][:, a0:clo], start=False, stop=True),
                     reads=[B_Lt[l], B_const], writes=[B_PS[z]])
            cc = slice(clo, TW)
            S.op("pe", lambda: nc.tensor.matmul(PS[z][:, cc], lhsT=tri8[:, :], rhs=Lt[l][:, cc], start=False, stop=False),
                 reads=[B_Lt[l], B_const], writes=[B_PS[z]])
            S.op("pe", lambda: nc.tensor.matmul(PS[z][:, cc], lhsT=one8[:, :], rhs=Lb[lp][:, cc], start=False, stop=True),
                 reads=[B_Lb[lp], B_const], writes=[B_PS[z]])
        S.op("act", lambda: nc.scalar.activation(out=At[l][:, cols], in_=PS[z][:, cols], func=AF.Exp, bias=bias_h[:, h:h + 1], scale=0.125),
             reads=[B_PS[z], B_biash], writes=[B_At[l]])
        if st["diag"]:
            S.op("dve", lambda: nc.vector.tensor_tensor(out=At[l][:, a0:a0 + 128], in0=At[l][:, a0:a0 + 128], in1=trim[:, :], op=ALU.mult),
                 reads=[B_At[l], B_const], writes=[B_At[l]])

    def att_s3(k, st):
        l = k % 3
        a0, j, oa, pr, qi = st["a0"], st["j"], st["oa"], st["pr"], st["qi"]
        S.op("pe", lambda: nc.tensor.matmul(PS[oa][:, a0:TW], lhsT=Vc[:, j, :], rhs=At[l][:, a0:TW], start=st["first"], stop=st["last"],
                                            skip_group_check=True), reads=[B_Vc, B_At[l]], writes=[B_PS[oa]])
        if st["last"]:
            S.op("dve", lambda: nc.vector.tensor_copy(out=OTc[pr, qi * TW:(qi + 1) * TW], in_=PS[oa][pr, :]),
                 reads=[B_PS[oa]], writes=[B_OTc])

    if stage >= 2:
        for c in range(KC):
            S.dma("sp", KTc[:, :], kTs[c], reads=[B_kTs[c]], writes=[B_KTc])
            S.dma("sp", QTc[:, :], qTs[c], reads=[B_qTs[c]], writes=[B_QTc])
            for b0 in range(0, NBLK, 4):
                S.dma("sp", Vc[:, b0:b0 + 4, :],
                      vtok[b0 * 128:(b0 + 4) * 128, c * 128:(c + 1) * 128].rearrange("(b p) f -> p b f", p=128),
                      reads=[B_vtok], writes=[B_Vc])
            steps = []
            for hp in range(2):
                for qi in range(NT):
                    oa = 2 + unitc[0] % 2
                    unitc[0] += 1
                    jmax = 4 * qi + 3
                    for j in range(jmax, -1, -1):
                        bq = max(0, j - 4 * qi)
                        a0 = bq * 128
                        diag = j >= 4 * qi
                        first = j == jmax
                        carry_lo = a0 + 128 if diag else a0
                        steps.append(dict(h=2 * c + hp, pr=slice(hp * 64, hp * 64 + 64), qi=qi, j=j, a0=a0, diag=diag, first=first,
                                          last=(j == 0), carry_lo=carry_lo, has_carry=((not first) and carry_lo < TW), oa=oa))
            n = len(steps)
            for i in range(n + 2):
                if i < n:
                    att_s1(i, steps[i])
                if 0 <= i - 1 < n:
                    att_s2(i - 1, steps[i - 1])
                if 0 <= i - 2 < n:
                    att_s3(i - 2, steps[i - 2])
            S.dma("pool", oTs[c], OTc[:, :], reads=[B_OTc], writes=[B_oTs[c]])

    sa.close()
    S.barrier()

    def layer1_tail(W, oT_in, B_oin, out_fn, nblk, rows):
        for jj in range(2):
            wt, Bw = w_next(("wo", jj))
            for oc4 in range(4):
                oc = jj * 4 + oc4
                p = next_ps()
                mm_group(p, W, wt, Bw, oc4 * 128, oT_in, B_oin, KC)
                S.op("dve", lambda oc=oc, p=p: nc.vector.tensor_tensor(out=xT[:, oc, 0:W], in0=xT[:, oc, 0:W], in1=PS[p][:, 0:W], op=ALU.add),
                     reads=[B_PS[p], B_xT[oc]], writes=[B_xT[oc]])
        rmsnorm(xT, B_xT, W, "ffn_g", 1, hT[1], B_hT[1])
        ffn(1, W, hT[1], B_hT[1], xT, B_xT)
        rmsnorm(xT, B_xT, W, "fin_g", 0, acc, B_acc)
        store_T_tokmajor(acc, B_acc, W, out_fn, nblk, rows)

    if stage >= 3:
        for t in range(NT):
            c0 = t * TW
            for c in range(KC):
                S.dma("sp", xT[:, c, :], x1T[c, :, c0:c0 + TW], reads=[B_x1T[c]], writes=[B_xT[c]])
                S.dma("sp", hT[0][:, c, :], oTs[c, :, c0:c0 + TW], reads=[B_oTs[c]], writes=[B_hT[0][c]])
            layer1_tail(TW, hT[0], B_hT[0], lambda b, c0=c0: yp[c0 + b * 128:c0 + (b + 1) * 128, :], 4, 128)


    B_osT = Buf("osT")
    if sample and stage >= 4:
        ss = ExitStack()
        Kpg = [sb("Kpg%d" % i, [128, D], F32, ss) for i in range(2)]
        Vpg = [sb("Vpg%d" % i, [128, D], F32, ss) for i in range(2)]
        KpT = sb("KpT", [128, KC, 128], BF16, ss)
        Vpb = sb("Vpb", [128, D], BF16, ss)
        vsn = sb("vsn", [DSQ, NSS, D], BF16, ss)
        Qbd = sb("Qbd", [128, KC, NSS, 16], BF16, ss)
        ptb = sb("ptb", [128, NSS * NPG], I32, ss)
        ptf = sb("ptf", [128, NSS * NPG], F32, ss)
        iot = sb("iot", [128, 1], F32, ss)
        idx = sb("idx", [128, NSS * NPG], I32, ss)
        onef = sb("onef", [1, 128], F32, ss)
        bf32 = sb("bf32", [1, 128], F32, ss)
        lo32 = sb("lo32", [1, 128], F32, ss)
        bhi = sb("bhi", [1, 128], BF16, ss)
        blo = sb("blo", [1, 128], BF16, ss)
        one1 = sb("one1", [1, 128], BF16, ss)
        mask8 = sb("mask8", [DSQ, 128], BF16, ss)
        Es = [sb("Es%d" % i, [128, 128], F32, ss) for i in range(2)]
        Ls = [sb("Ls%d" % i, [128, 128], BF16, ss) for i in range(2)]
        As = [sb("As%d" % i, [128, 128], BF16, ss) for i in range(2)]
        Lbs = [sb("Lbs%d" % i, [128, 128], BF16, ss) for i in range(2)]
        Lac = sb("Lac", [128, 128], F32, ss)
        B_Kpg = [Buf("Kpg%d" % i) for i in range(2)]
        B_Vpg = [Buf("Vpg%d" % i) for i in range(2)]
        B_KpT, B_Vpb, B_vsn, B_Qbd, B_idx = Buf("KpT"), Buf("Vpb"), Buf("vsn"), Buf("Qbd"), Buf("idx")
        B_ptb, B_ptf, B_iot, B_bias = Buf("ptb"), Buf("ptf"), Buf("iot"), Buf("sbias")
        B_Es = [Buf("Es%d" % i) for i in range(2)]
        B_Ls = [Buf("Ls%d" % i) for i in range(2)]
        B_As = [Buf("As%d" % i) for i in range(2)]
        B_Lbs = [Buf("Lbs%d" % i) for i in range(2)]
        B_Lac = Buf("Lac")
        S.op("pool", lambda: g.iota(iot[:], pattern=[[0, 1]], base=0, channel_multiplier=1, allow_small_or_imprecise_dtypes=True),
             writes=[B_iot])
        S.dma("sp", ptb[:], pt.rearrange("i j -> (i j)").partition_broadcast(128), writes=[B_ptb])
        S.op("dve", lambda: nc.vector.tensor_copy(out=ptf[:], in_=ptb[:]), reads=[B_ptb], writes=[B_ptf])
        S.op("dve", lambda: nc.vector.tensor_scalar(out=ptf[:], in0=ptf[:], scalar1=128.0, scalar2=iot[:, 0:1], op0=ALU.mult, op1=ALU.add),
             reads=[B_ptf, B_iot], writes=[B_ptf])
        S.op("dve", lambda: nc.vector.tensor_copy(out=idx[:], in_=ptf[:]), reads=[B_ptf], writes=[B_idx])
        S.op("dve", lambda: nc.vector.memset(onef[:], 1.0), writes=[B_bias])
        S.op("dve", lambda: nc.vector.memset(one1[:], 1.0), reads=[B_bias], writes=[B_bias])
        for h in range(NH):
            S.op("dve", lambda h=h: nc.vector.tensor_scalar(out=bf32[0:1, h * 8:(h + 1) * 8], in0=onef[0:1, 0:8],
                                                            scalar1=bias_h[0:1, h:h + 1], scalar2=8.0, op0=ALU.mult, op1=ALU.mult),
                 reads=[B_biash, B_bias], writes=[B_bias])
        S.op("dve", lambda: nc.vector.tensor_copy(out=bhi[:], in_=bf32[:]), reads=[B_bias], writes=[B_bias])
        S.op("dve", lambda: nc.vector.tensor_tensor(out=lo32[:], in0=bf32[:], in1=bhi[:], op=ALU.subtract), reads=[B_bias], writes=[B_bias])
        S.op("dve", lambda: nc.vector.tensor_copy(out=blo[:], in_=lo32[:]), reads=[B_bias], writes=[B_bias])
        for h in range(NH):
            S.op("dve", lambda h=h: nc.vector.tensor_copy(out=mask8[:, h * 8:(h + 1) * 8], in_=trim[0:DSQ, 0:DSQ]),
                 reads=[B_const, B_bias], writes=[B_bias])
        S.op("dve", lambda: nc.vector.memset(Qbd[:], 0.0), writes=[B_Qbd])
        qv = qsT[:].rearrange("p k (i t) -> p k i t", t=DSQ)
        S.op("dve", lambda: nc.vector.tensor_copy(out=Qbd[0:64, :, :, 0:8], in_=qv[0:64, :, :, :]), reads=[B_qsT, B_Qbd], writes=[B_Qbd])
        S.op("dve", lambda: nc.vector.tensor_copy(out=Qbd[64:128, :, :, 8:16], in_=qv[64:128, :, :, :]), reads=[B_qsT, B_Qbd], writes=[B_Qbd])
        for i in range(NSS):
            for c4 in range(2):
                p = next_ps(4, 7)
                for cc in range(4):
                    c = c4 * 4 + cc
                    S.op("pe", lambda c=c, cc=cc, p=p, i=i: nc.tensor.transpose(PS[p][0:DSQ, cc * 128:(cc + 1) * 128],
                                                                                vs32[:, c, i * DSQ:(i + 1) * DSQ], ident[:, :]),
                         reads=[B_vs32[c], B_const], writes=[B_PS[p]], inc=(cc == 3))
                S.op("dve", lambda c4=c4, p=p, i=i: nc.vector.tensor_copy(out=vsn[0:DSQ, i, c4 * 512:(c4 + 1) * 512], in_=PS[p][0:DSQ, :]),
                     reads=[B_PS[p]], writes=[B_vsn])

        sstep = [0]
        for i in range(NSS):
            S.op("dve", lambda: nc.vector.memset(Lac[:], 0.0), reads=[B_Lac], writes=[B_Lac])
            for q in range(2):
                S.op("dve", lambda q=q: nc.vector.memset(Lbs[q][:], 0.0), reads=[B_Lbs[q]], writes=[B_Lbs[q]])
            prev = 0
            nsteps = NPG + 1
            for st_i in range(nsteps):
                first = st_i == 0
                last = st_i == nsteps - 1
                z = sstep[0] % 2
                bi = z
                sstep[0] += 1
                if first:
                    nk = DSQ
                    kT_fn = lambda kc, i=i: ksT[:, kc, i * DSQ:(i + 1) * DSQ]
                    kreads = [B_ksT]
                    v_ap = vsn[0:DSQ, i, :]
                    vreads = [B_vsn]
                else:
                    nk = 128
                    j = NPG - st_i
                    col = i * NPG + j
                    pb = sstep[0] % 2
                    S.dma("pool", Kpg[pb][:, :], ck[:, :], reads=[B_idx], writes=[B_Kpg[pb]],
                          indirect=bass.IndirectOffsetOnAxis(ap=idx[:, col:col + 1], axis=0))
                    S.dma("pool", Vpg[pb][:, :], cv[:, :], reads=[B_idx], writes=[B_Vpg[pb]],
                          indirect=bass.IndirectOffsetOnAxis(ap=idx[:, col:col + 1], axis=0))
                    for c4 in range(2):
                        p = next_ps(4, 7)
                        for cc in range(4):
                            c = c4 * 4 + cc
                            S.op("pe", lambda c=c, cc=cc, p=p, pb=pb: nc.tensor.transpose(PS[p][:, cc * 128:(cc + 1) * 128],
                                                                                          Kpg[pb][:, c * 128:(c + 1) * 128], ident[:, :]),
                                 reads=[B_Kpg[pb], B_const], writes=[B_PS[p]], inc=(cc == 3))
                        S.op("dve", lambda c4=c4, p=p: nc.vector.tensor_copy(
                            out=KpT[:, c4 * 4:(c4 + 1) * 4, :].rearrange("p a b -> p (a b)"), in_=PS[p][:, :]),
                            reads=[B_PS[p]], writes=[B_KpT])
                    S.op("act", lambda pb=pb: nc.scalar.copy(out=Vpb[:, :], in_=Vpg[pb][:, :]), reads=[B_Vpg[pb]], writes=[B_Vpb])
                    kT_fn = lambda kc: KpT[:, kc, :]
                    kreads = [B_KpT]
                    v_ap = Vpb[:, :]
                    vreads = [B_Vpb]
                for kc in range(KC):
                    S.op("pe", lambda kc=kc, z=z, nk=nk, kT_fn=kT_fn, i=i: nc.tensor.matmul(
                        PS[z][0:nk, kc * 16:(kc + 1) * 16], lhsT=kT_fn(kc), rhs=Qbd[:, kc, i, :],
                        start=(kc == 0), stop=False, skip_group_check=True),
                        reads=kreads + [B_Qbd], writes=[B_PS[z]], inc=(kc == KC - 1))
                for brow_t in (bhi, blo):
                    S.op("pe", lambda z=z, nk=nk, brow_t=brow_t: nc.tensor.matmul(
                        PS[z][0:nk, 0:128], lhsT=one1[0:1, 0:nk], rhs=brow_t[0:1, :], start=False, stop=False, skip_group_check=True),
                        reads=[B_bias], writes=[B_PS[z]])
                S.op("act", lambda z=z, nk=nk, bi=bi: nc.scalar.activation(out=Es[bi][0:nk, :], in_=PS[z][0:nk, 0:128], func=AF.Exp, scale=0.125),
                     reads=[B_PS[z]], writes=[B_Es[bi]])
                S.op("act", lambda nk=nk, bi=bi: nc.scalar.activation(out=Ls[bi][0:nk, :], in_=Es[bi][0:nk, :], func=AF.Ln, bias=1.0, scale=1.0),
                     reads=[B_Es[bi]], writes=[B_Ls[bi]])
                if first:
                    S.op("dve", lambda bi=bi: nc.vector.tensor_tensor(out=Ls[bi][0:DSQ, :], in0=Ls[bi][0:DSQ, :], in1=mask8[:, :], op=ALU.mult),
                         reads=[B_Ls[bi], B_bias], writes=[B_Ls[bi]])
                S.op("pe", lambda z=z, nk=nk, bi=bi, first=first: nc.tensor.matmul(
                    PS[z][0:nk, 0:128], lhsT=tri8[0:nk, 0:nk], rhs=Ls[bi][0:nk, :], start=False, stop=first, skip_group_check=True),
                    reads=[B_Ls[bi], B_const], writes=[B_PS[z]])
                if not first:
                    S.op("pe", lambda z=z, nk=nk, prev=prev: nc.tensor.matmul(
                        PS[z][0:nk, 0:128], lhsT=one8[:, 0:nk], rhs=Lbs[prev][:, :], start=False, stop=True, skip_group_check=True),
                        reads=[B_Lbs[prev], B_const], writes=[B_PS[z]])
                if not last:
                    S.op("dve", lambda nk=nk, bi=bi: nc.vector.tensor_tensor(out=Lbs[bi][0:nk, :], in0=Lac[0:nk, :], in1=Ls[bi][0:nk, :], op=ALU.add),
                         reads=[B_Lac, B_Ls[bi]], writes=[B_Lbs[bi]])
                    S.op("dve", lambda nk=nk, bi=bi: nc.vector.tensor_tensor(out=Lac[0:nk, :], in0=Lac[0:nk, :], in1=Ls[bi][0:nk, :], op=ALU.add),
                         reads=[B_Lac, B_Ls[bi]], writes=[B_Lac])
                S.op("act", lambda z=z, nk=nk, bi=bi: nc.scalar.activation(out=As[bi][0:nk, :], in_=PS[z][0:nk, 0:128], func=AF.Exp, scale=0.125),
                     reads=[B_PS[z]], writes=[B_As[bi]])
                if first:
                    S.op("dve", lambda bi=bi: nc.vector.tensor_tensor(out=As[bi][0:DSQ, :], in0=As[bi][0:DSQ, :], in1=mask8[:, :], op=ALU.mult),
                         reads=[B_As[bi], B_bias], writes=[B_As[bi]])
                for kc in range(KC):
                    S.op("pe", lambda kc=kc, nk=nk, bi=bi, v_ap=v_ap, first=first, last=last: nc.tensor.matmul(
                        PS[2 + kc // 4][:, (kc % 4) * 128:(kc % 4 + 1) * 128], lhsT=v_ap[:, kc * 128:(kc + 1) * 128], rhs=As[bi][0:nk, :],
                        start=(first and kc % 4 == 0), stop=last, skip_group_check=True),
                        reads=vreads + [B_As[bi]], writes=[B_PS[2 + kc // 4]], inc=(kc % 4 == 3))
                prev = bi
            for kc in range(KC):
                for hp in range(2):
                    h = 2 * kc + hp
                    S.op("dve", lambda kc=kc, hp=hp, h=h, i=i: nc.vector.tensor_copy(
                        out=osT[hp * 64:(hp + 1) * 64, kc, i * DSQ:(i + 1) * DSQ],
                        in_=PS[2 + kc // 4][hp * 64:(hp + 1) * 64, (kc % 4) * 128 + h * 8:(kc % 4) * 128 + h * 8 + 8]),
                        reads=[B_PS[2 + kc // 4]], writes=[B_osT])
        ss.close()
        S.op("dve", lambda: nc.vector.tensor_copy(out=xT[:, :, 0:NS], in_=xsT[:, :, :]), reads=[B_xsT] + B_xT, writes=B_xT)
        layer1_tail(NS, osT, [B_osT] * KC, lambda b: ys[:, :], 1, NS)

    es.close()
    S.finish("sp")
    S.close()
    return nc


def make_in_maps(inputs, n_cores=8):
    f = lambda a: np.ascontiguousarray(np.asarray(a))
    x_prompt = f(inputs["x_prompt"])
    x_sample = f(inputs["x_sample"])
    cache_conv = f(inputs["cache_conv"])
    ckf = f(inputs["cache_k"])
    cvf = f(inputs["cache_v"])
    npool = ckf.shape[1]
    ck2 = ckf.reshape(npool * 128, D)
    cv2 = cvf.reshape(npool * 128, D)
    page_table = f(inputs["page_table"]).astype(np.int32)
    shared = {
        "ck": ck2, "cv": cv2,
        "mix_g": f(inputs["mix_norm_g"]), "ffn_g": f(inputs["ffn_norm_g"]), "fin_g": f(inputs["final_norm_g"]).reshape(1, D),
        "b_pw1": f(inputs["cv_b_pw1"]).reshape(2, D), "b_dw": f(inputs["cv_b_dw"]).reshape(1, D),
        "ln_g": f(inputs["cv_ln_g"]).reshape(1, D), "ln_b": f(inputs["cv_ln_b"]).reshape(1, D),
        "b_pw2": f(inputs["cv_b_pw2"]).reshape(1, D), "w_dw": f(inputs["cv_w_dw"]).reshape(CW, D),
        "sbb": f(inputs["sb_logit_bias"]).reshape(1, NH),
        "w_pw1": f(inputs["cv_w_pw1"])[0], "w_pw2": f(inputs["cv_w_pw2"])[0], "w_qkv": f(inputs["sb_w_qkv"])[0],
        "w_o": f(inputs["sb_w_o"])[0], "wg": f(inputs["ffn_w_gate"]), "wu": f(inputs["ffn_w_up"]), "wd": f(inputs["ffn_w_down"]),
    }
    in_maps = []
    for c in range(n_cores):
        m = dict(shared)
        m["xp"] = x_prompt[c // 2]
        m["xs"] = x_sample[NSS * c:NSS * (c + 1)].reshape(NS, D)
        m["cconv"] = cache_conv[0, NSS * c:NSS * (c + 1)].reshape(NSS * CS, D)
        m["pt"] = page_table[NSS * c:NSS * (c + 1)]
        in_maps.append(m)
    return in_maps


def kernel(**inputs):
    x_prompt = np.asarray(inputs["x_prompt"])
    BATCH, SEQ, _ = x_prompt.shape
    NPG = np.asarray(inputs["page_table"]).shape[1]
    NPOOL = np.asarray(inputs["cache_k"]).shape[1]
    nc = build(SEQ, NPG, NPOOL)
    in_maps = make_in_maps(inputs)
    res = run_bass_kernel_spmd(nc, in_maps, core_ids=list(range(8))).results
    ev = [res[2 * b] for b in range(BATCH)]
    y_prompt = np.stack([r["yp"] for r in ev]).astype(np.float32)
    y_sample = np.concatenate([r["ys"].reshape(NSS, DSQ, D) for r in res]).astype(np.float32)
    conv_p = np.stack([r["csp"] for r in ev])[None].astype(np.float32)
    conv_s = np.concatenate([r["css"].reshape(NSS, CS, D) for r in res])[None].astype(np.float32)
    k_p = np.stack([r["kp"].reshape(SEQ // 128, 128, NH, 64) for r in ev])[None].astype(np.float32)
    v_p = np.stack([r["vp"].reshape(SEQ // 128, 128, NH, 64) for r in ev])[None].astype(np.float32)
    k_s = np.concatenate([r["ks"].reshape(NSS, DSQ, NH, 64) for r in res])[None].astype(np.float32)
    v_s = np.concatenate([r["vs"].reshape(NSS, DSQ, NH, 64) for r in res])[None].astype(np.float32)
    return (y_prompt, y_sample, conv_p, conv_s, k_p, v_p, k_s, v_s)
```
